# Optimizing a Trainium2 kernel written in Bass

```python
import jax, jax.numpy as jnp
from jax import lax
import numpy as np

D_MODEL = 1024
BATCH = 8
SEQ = 4096
DEPTH = 1

CHUNK = 64
Q_BLOCK = 128

HEAD_DIM = 64
N_SB_HEADS = 8
D_SB = N_SB_HEADS * HEAD_DIM
N_MLA_HEADS = 8
QK_NOPE_DIM = 64
QK_ROPE_DIM = 32
V_HEAD_DIM = 64
Q_LORA_RANK = 256
KV_LORA_RANK = 128
D_MLA = N_MLA_HEADS * V_HEAD_DIM
D_MIX = D_SB + D_MLA
ROPE_THETA = 10000.0
PLE_DIM = 256
EPS = 1e-6

IN_SPLITS = (D_SB, D_SB, D_SB, D_SB, Q_LORA_RANK, KV_LORA_RANK, QK_ROPE_DIM, D_MLA)
D_IN = sum(IN_SPLITS)
IN_SPLIT_IDX = tuple(int(v) for v in np.cumsum(IN_SPLITS)[:-1])

kernel_name = "hybrid_stickbreak_mla_block"


def rms_norm(x, g):
    xf = x.astype(jnp.float32)
    y = xf * lax.rsqrt(jnp.mean(xf * xf, axis=-1, keepdims=True) + EPS)
    return (y * g.astype(jnp.float32)).astype(x.dtype)


def head_rms_norm(o, g):
    B, S, H, d = o.shape
    return rms_norm(o, g.reshape(H, d)).reshape(B, S, H * d)


def to_blocks(a):
    B, S = a.shape[:2]
    return a.reshape(B, S // Q_BLOCK, Q_BLOCK, *a.shape[2:]).swapaxes(0, 1)


def from_blocks(a):
    a = a.swapaxes(0, 1)
    return a.reshape(a.shape[0], a.shape[1] * a.shape[2], *a.shape[3:])


def apply_rope(x, positions):
    half = x.shape[-1] // 2
    freq = ROPE_THETA ** (-jnp.arange(half, dtype=jnp.float32) / half)
    ang = positions.astype(jnp.float32)[..., None] * freq
    ang = ang.reshape(ang.shape[:2] + (1,) * (x.ndim - 3) + (half,))
    cos, sin = jnp.cos(ang).astype(x.dtype), jnp.sin(ang).astype(x.dtype)
    x1, x2 = x[..., :half], x[..., half:]
    return jnp.concatenate([x1 * cos - x2 * sin, x2 * cos + x1 * sin], axis=-1)


def stick_breaking_attention(q, k, v):
    S = k.shape[1]
    scale = HEAD_DIM ** -0.5
    key_idx = jnp.arange(S)

    def block(args):
        b_idx, q_blk = args
        z = jnp.einsum('bqhd,bkhd->bhqk', q_blk, k).astype(jnp.float32) * scale
        t_idx = b_idx * Q_BLOCK + jnp.arange(Q_BLOCK)
        past = key_idx[None, :] < t_idx[:, None]
        log_fail = jnp.where(past, jax.nn.log_sigmoid(-z), 0.0)
        suffix = lax.cumsum(log_fail, axis=3, reverse=True) - log_fail
        w = jnp.where(past, jnp.exp(jax.nn.log_sigmoid(z) + suffix), 0.0)
        return jnp.einsum('bhqk,bkhd->bqhd', w.astype(v.dtype), v)

    out = lax.map(block, (jnp.arange(S // Q_BLOCK), to_blocks(q)))
    return from_blocks(out)


def latent_attention(q_nope, q_rope, k_nope, k_rope, v):
    S = k_nope.shape[1]
    scale = (QK_NOPE_DIM + QK_ROPE_DIM) ** -0.5
    key_chunk = jnp.arange(S) // CHUNK

    def block(args):
        b_idx, qn, qr = args
        z = (jnp.einsum('bqhd,bkhd->bhqk', qn, k_nope)
             + jnp.einsum('bqhr,bkr->bhqk', qr, k_rope)).astype(jnp.float32) * scale
        q_chunk = (b_idx * Q_BLOCK + jnp.arange(Q_BLOCK)) // CHUNK
        visible = key_chunk[None, :] <= q_chunk[:, None]
        z = jnp.where(visible, z, -jnp.inf)
        w = jax.nn.softmax(z, axis=-1)
        return jnp.einsum('bhqk,bkhd->bqhd', w.astype(v.dtype), v)

    out = lax.map(block, (jnp.arange(S // Q_BLOCK), to_blocks(q_nope), to_blocks(q_rope)))
    return from_blocks(out)


def setup_inputs(seed: int = 0) -> dict:
    key = jax.random.key(seed)
    ks = jax.random.split(key, 20)

    def w(k, shape, fan_in):
        return jax.random.normal(k, shape, jnp.float32) * fan_in ** -0.5

    def gain(k, n):
        return 1.0 + 0.05 * jax.random.normal(k, (DEPTH, n), jnp.float32)

    x = jax.random.normal(ks[0], (BATCH, SEQ, D_MODEL), jnp.float32)
    p = jax.random.normal(ks[1], (DEPTH, BATCH, SEQ, PLE_DIM), jnp.float32)
    start = jax.random.randint(ks[2], (BATCH, 1), 0, 4096, dtype=jnp.int32)
    positions = start + jnp.arange(SEQ, dtype=jnp.int32)[None, :]
    return {
        'x': x,
        'p': p,
        'positions': positions,
        'norm_pre_g': gain(ks[3], D_MODEL),
        'w_in': w(ks[4], (DEPTH, D_MODEL, D_IN), D_MODEL),
        'q_norm_g': gain(ks[5], Q_LORA_RANK),
        'w_uq': w(ks[6], (DEPTH, Q_LORA_RANK, N_MLA_HEADS * (QK_NOPE_DIM + QK_ROPE_DIM)), Q_LORA_RANK),
        'kv_norm_g': gain(ks[7], KV_LORA_RANK),
        'w_ukv': w(ks[8], (DEPTH, KV_LORA_RANK, N_MLA_HEADS * (QK_NOPE_DIM + V_HEAD_DIM)), KV_LORA_RANK),
        'sb_out_norm_g': gain(ks[9], D_SB),
        'mla_out_norm_g': gain(ks[10], D_MLA),
        'w_out': w(ks[11], (DEPTH, D_MIX, D_MODEL), D_MIX),
        'norm_post_g': gain(ks[12], D_MODEL),
        'w_ple': w(ks[13], (DEPTH, PLE_DIM, D_MODEL), PLE_DIM),
        'ple_norm_g': gain(ks[14], D_MODEL),
        'w_ple_gate': w(ks[15], (DEPTH, D_MODEL, D_MODEL), D_MODEL),
        'b_ple_gate': 0.02 * jax.random.normal(ks[16], (DEPTH, D_MODEL), jnp.float32),
    }


def reference(x, p, positions, norm_pre_g, w_in, q_norm_g, w_uq, kv_norm_g, w_ukv,
              sb_out_norm_g, mla_out_norm_g, w_out, norm_post_g, w_ple, ple_norm_g,
              w_ple_gate, b_ple_gate):
    B, S, _ = x.shape
    for i in range(DEPTH):
        h = rms_norm(x, norm_pre_g[i])
        proj = h @ w_in[i]
        sb_q, sb_k, sb_v, sb_g, c_q, c_kv, k_rope, mla_g = jnp.split(proj, IN_SPLIT_IDX, axis=-1)

        sb_o = stick_breaking_attention(sb_q.reshape(B, S, N_SB_HEADS, HEAD_DIM),
                                        sb_k.reshape(B, S, N_SB_HEADS, HEAD_DIM),
                                        sb_v.reshape(B, S, N_SB_HEADS, HEAD_DIM))
        sb_y = head_rms_norm(sb_o, sb_out_norm_g[i]) * jax.nn.silu(sb_g)

        q = (rms_norm(c_q, q_norm_g[i]) @ w_uq[i]).reshape(B, S, N_MLA_HEADS, QK_NOPE_DIM + QK_ROPE_DIM)
        q_nope, q_rope = q[..., :QK_NOPE_DIM], apply_rope(q[..., QK_NOPE_DIM:], positions)
        kv = (rms_norm(c_kv, kv_norm_g[i]) @ w_ukv[i]).reshape(B, S, N_MLA_HEADS, QK_NOPE_DIM + V_HEAD_DIM)
        k_nope, v = kv[..., :QK_NOPE_DIM], kv[..., QK_NOPE_DIM:]
        k_rope = apply_rope(k_rope, positions)
        mla_o = latent_attention(q_nope, q_rope, k_nope, k_rope, v)
        mla_y = head_rms_norm(mla_o, mla_out_norm_g[i]) * jax.nn.silu(mla_g)

        y = jnp.concatenate([sb_y, mla_y], axis=-1) @ w_out[i]
        x = x + rms_norm(y, norm_post_g[i])

        ple = rms_norm(p[i] @ w_ple[i], ple_norm_g[i])
        x = x + ple * jax.nn.sigmoid(x @ w_ple_gate[i] + b_ple_gate[i])
    return x
```

```python
import numpy as np
from contextlib import ExitStack
import concourse.bass as bass
import concourse.mybir as mybir
from concourse.bass_utils import run_bass_kernel_spmd

F32 = mybir.dt.float32
BF16 = mybir.dt.bfloat16
I32 = mybir.dt.int32
AF = mybir.ActivationFunctionType
ALU = mybir.AluOpType
AX = mybir.AxisListType

S = 4096
D = 1024
NB = S // 128
NCH = S // 512
EPS = 1e-6
DBG = False
SEQ5 = False
SEQ3 = False


class Prog:
    ENG = ("pe", "act", "dve", "pool", "sp")

    def __init__(self, nc, stack):
        self.nc = nc
        self.stack = stack
        self.lists = {e: [] for e in self.ENG}
        self.sem = {e: stack.enter_context(nc.semaphore("s_" + e)) for e in self.ENG}
        self.cnt = {e: 0 for e in self.ENG}
        self.waited = {e: {} for e in self.ENG}
        self.lastw = {}
        self.reads = {}
        self.semobj = {e: self.sem[e] for e in self.ENG}
        self.dma_cnt = {}
        self.capture = None
        self.fence_fn = {}
        self.cur_gid = None
        self.gid_ctr = 0

    def new_dma_sem(self, name):
        s = self.stack.enter_context(self.nc.semaphore(name))
        self.semobj[name] = s
        self.dma_cnt[name] = 0
        return name

    def _deps(self, reads, writes):
        deps = {}

        def add(k, v):
            if deps.get(k, 0) < v:
                deps[k] = v
        for r in reads:
            if r in self.lastw:
                add(*self.lastw[r])
        for w in writes:
            if w in self.lastw:
                add(*self.lastw[w])
            for k, v in self.reads.get(w, {}).items():
                add(k, v)
        return deps

    def _emit_waits(self, eng, deps):
        for k, v in deps.items():
            if k == eng and eng == "pe":
                continue
            if self.waited[eng].get(k, 0) >= v:
                continue
            self.waited[eng][k] = v
            so = self.semobj[k]
            self.lists[eng].append(lambda e, so=so, v=v: e.wait_ge(so, v))

    def _record(self, key, val, reads, writes):
        for r in reads:
            d = self.reads.setdefault(r, {})
            if d.get(key, 0) < val:
                d[key] = val
        for w in writes:
            self.lastw[w] = (key, val)
            self.reads[w] = {}

    def op(self, eng, fn, reads=(), writes=(), fence=False):
        if self.capture is not None:
            self.capture.append((self.cur_gid, "op", eng, fn, list(reads), list(writes), fence))
            return
        self._emit_waits(eng, self._deps(reads, writes))
        self.cnt[eng] += 1
        so = self.sem[eng]
        self.lists[eng].append(lambda e, fn=fn, so=so: fn(e).then_inc(so, 1))
        if fence and self.fence_fn.get(eng) is not None:
            self.cnt[eng] += 1
            ff = self.fence_fn[eng]
            self.lists[eng].append(lambda e, ff=ff, so=so: ff(e).then_inc(so, 1))
        self._record(eng, self.cnt[eng], reads, writes)

    def dma(self, eng, semname, fn, reads=(), writes=()):
        if self.capture is not None:
            self.capture.append((self.cur_gid, "dma", eng, semname, fn, list(reads), list(writes)))
            return
        self._emit_waits(eng, self._deps(reads, writes))
        self.dma_cnt[semname] += 16
        so = self.semobj[semname]
        self.lists[eng].append(lambda e, fn=fn, so=so: fn(e).then_inc(so, 16))
        self._record(semname, self.dma_cnt[semname], reads, writes)

    def collect(self, fn, *args):
        self.capture = []
        fn(*args)
        ops, self.capture = self.capture, None
        return ops

    def group_begin(self):
        self.gid_ctr += 1
        self.cur_gid = self.gid_ctr

    def group_end(self):
        self.cur_gid = None

    def _emit_item(self, it):
        if it[1] == "op":
            self.op(*it[2:])
        else:
            self.dma(*it[2:])

    def replay_interleaved(self, lists):
        lists = [l for l in lists if l]
        idx = [0] * len(lists)
        total = max(len(l) for l in lists) if lists else 0
        for step in range(total):
            for li, l in enumerate(lists):
                hi = (step + 1) * len(l) // total
                while idx[li] < hi:
                    it = l[idx[li]]
                    idx[li] += 1
                    self._emit_item(it)
                    if it[0] is not None:
                        while idx[li] < len(l) and l[idx[li]][0] == it[0]:
                            self._emit_item(l[idx[li]])
                            idx[li] += 1

    def wait_all(self, eng, res):
        self._emit_waits(eng, self._deps(res, res))

    def barrier(self):
        allv = {e: self.cnt[e] for e in self.ENG if self.cnt[e] > 0}
        for k, v in self.dma_cnt.items():
            if v > 0:
                allv[k] = v
        for e in self.ENG:
            self._emit_waits(e, {k: v for k, v in allv.items() if k != e})

    def finish(self):
        with self.nc.Block() as block:
            @block.tensor
            def _(e):
                for f in self.lists["pe"]:
                    f(e)

            @block.scalar
            def _(e):
                for f in self.lists["act"]:
                    f(e)

            @block.vector
            def _(e):
                for f in self.lists["dve"]:
                    f(e)

            @block.gpsimd
            def _(e):
                for f in self.lists["pool"]:
                    f(e)

            @block.sync
            def _(e):
                for f in self.lists["sp"]:
                    f(e)


class Region:
    def __init__(self, arena, start, end):
        self.arena, self.start, self.end, self.ptr = arena, start, end, start

    def reset(self):
        self.ptr = self.start

    def alloc(self, shape, dt, parts=128):
        esz = 4 if dt in (F32, I32) else 2
        n = 1
        for s_ in shape[1:]:
            n *= s_
        nbytes = (n * esz + 31) // 32 * 32
        off = self.ptr
        assert off + nbytes <= self.end, ("region overflow", shape, off, nbytes, self.end)
        self.ptr += nbytes
        v = self.arena[0:shape[0], off // 2: off // 2 + (n * esz) // 2]
        if esz == 4:
            v = v.bitcast(dt)
        if len(shape) == 3:
            v = v.rearrange("p (a b) -> p a b", b=shape[2])
        elif len(shape) == 4:
            v = v.rearrange("p (a b c) -> p a b c", b=shape[2], c=shape[3])
        return v


def build():
    nc = bass.Bass("TRN2", target_bir_lowering=False)

    def din(name, shape, dt=F32):
        return nc.dram_tensor(name, shape, dt, kind="ExternalInput").ap()
    x_d = din("x", [S, D])
    p_d = din("p", [S, 256])
    pos_d = din("pos", [S], I32)
    freq_d = din("freq", [32, 1])
    g_pre_d = din("norm_pre_g", [D])
    w_in_d = din("w_in", [D, 2976])
    g_q_d = din("q_norm_g", [256])
    w_uq_d = din("w_uq", [256, 768])
    g_kv_d = din("kv_norm_g", [128])
    w_ukv_d = din("w_ukv", [128, 1024])
    g_sbo_d = din("sb_out_norm_g", [512])
    g_mlo_d = din("mla_out_norm_g", [512])
    w_out_d = din("w_out", [D, D])
    g_post_d = din("norm_post_g", [D])
    w_ple_d = din("w_ple", [256, D])
    g_ple_d = din("ple_norm_g", [D])
    w_pg_d = din("w_ple_gate", [D, D])
    b_pg_d = din("b_ple_gate", [D])
    out_d = nc.dram_tensor("out", [S, D], F32, kind="ExternalOutput").ap()
    if DBG:
        dbg_sb_d = nc.dram_tensor("dbg_sb", [S, 512], F32, kind="ExternalOutput").ap()
        dbg_ml_d = nc.dram_tensor("dbg_ml", [S, 512], F32, kind="ExternalOutput").ap()
        dbg_x1_d = nc.dram_tensor("dbg_x1", [S, D], F32, kind="ExternalOutput").ap()
        dbg_pl_d = nc.dram_tensor("dbg_pl", [S, D], F32, kind="ExternalOutput").ap()
        dbg_sg_d = nc.dram_tensor("dbg_sg", [S, D], F32, kind="ExternalOutput").ap()

    with ExitStack() as st:
        P = Prog(nc, st)
        ARENA_B = 204 * 1024
        arena_t = st.enter_context(nc.sbuf_tensor("arena", [128, ARENA_B // 2], BF16))
        arena = arena_t[:]
        ps_all = st.enter_context(nc.psum_tensor("ps_all", [128, 4096], F32))
        psb = [ps_all[:, i * 512:(i + 1) * 512] for i in range(8)]
        K = 1024
        R_const = Region(arena, 0, 4 * K)
        R_osb = Region(arena, 4 * K, 36 * K)
        R_oml = Region(arena, 36 * K, 68 * K)
        R_big = Region(arena, 68 * K, 164 * K)
        R_tmp = Region(arena, 164 * K, 204 * K)
        R_o2 = Region(arena, 4 * K, 68 * K)

        def MM(out, lhsT, rhs, start, stop, r, w):
            P.op("pe", lambda e: e.matmul(out, lhsT=lhsT, rhs=rhs, start=start, stop=stop), r, w)

        def TR(out, in_, r, w):
            P.op("pe", lambda e: e.transpose(out=out, in_=in_, identity=ident), list(r) + ["ident"], w)

        def ACT(out, in_, func, r, w, scale=1.0, bias=0.0, accum=None):
            P.op("act", lambda e: e.activation(out=out, in_=in_, func=func, bias=bias, scale=scale,
                                               accum_out=accum), r, w, fence=(accum is not None))

        def TS(eng, out, in0, s1, s2, op0, op1, r, w, fence=False):
            if s2 is None:
                P.op(eng, lambda e: e.tensor_scalar(out=out, in0=in0, scalar1=s1, scalar2=None, op0=op0), r, w, fence=fence)
            else:
                P.op(eng, lambda e: e.tensor_scalar(out=out, in0=in0, scalar1=s1, scalar2=s2, op0=op0, op1=op1), r, w, fence=fence)

        def TT(eng, out, in0, in1, op, r, w, fence=False):
            P.op(eng, lambda e: e.tensor_tensor(out=out, in0=in0, in1=in1, op=op), r, w, fence=fence)

        def STT(eng, out, in0, scalar, in1, op0, op1, r, w):
            P.op(eng, lambda e: e.scalar_tensor_tensor(out=out, in0=in0, scalar=scalar, in1=in1, op0=op0, op1=op1), r, w)

        def CP(eng, out, in_, r, w):
            if eng == "act":
                P.op("act", lambda e: e.copy(out=out, in_=in_), r, w)
            else:
                P.op(eng, lambda e: e.tensor_copy(out=out, in_=in_), r, w)

        def MSET(eng, ap, val, w):
            P.op(eng, lambda e: e.memset(ap, val), [], w)

        fsc = {e: R_const.alloc([128, 8], F32) for e in ("act", "dve", "pool")}
        rsq_tmp = [R_const.alloc([128, 16], F32) for _ in range(4)]
        rsq_i = [0]

        def RSQ(out, in_, c, n, r, w):
            k = rsq_i[0] % 4
            rsq_i[0] += 1
            t = rsq_tmp[k][0:in_.shape[0], 0:n]
            own = P.cur_gid is None
            if own:
                P.group_begin()
            TS("pool", t, in_, float(c), None, ALU.add, None, r, ["rsqt%d" % k], fence=True)
            TT("pool", out, t, neghalf[0:in_.shape[0], 0:n], ALU.pow, ["rsqt%d" % k, "neghalf"], w, fence=True)
            if own:
                P.group_end()

        def RSQ2(out, a, b, c, r, w):
            k = rsq_i[0] % 4
            rsq_i[0] += 1
            t = rsq_tmp[k][:, 0:1]
            t2 = rsq_tmp[k][:, 8:9]
            own = P.cur_gid is None
            if own:
                P.group_begin()
            TT("pool", t, a, b, ALU.add, r, ["rsqt%d" % k], fence=True)
            TS("pool", t2, t, float(c), None, ALU.add, None, ["rsqt%d" % k], ["rsqu%d" % k], fence=True)
            TT("pool", out, t2, neghalf[:, 0:1], ALU.pow, ["rsqu%d" % k, "neghalf"], w, fence=True)
            if own:
                P.group_end()

        def DMA(eng, sem, out, in_, r, w, slow=False):
            if slow:
                P.dma(eng, sem, lambda e: e.dma_start(out=out, in_=in_, allow_slow_non_contiguous=True), r, w)
            else:
                P.dma(eng, sem, lambda e: e.dma_start(out=out, in_=in_), r, w)

        def pbf(i):
            return psb[i][:].bitcast(BF16)

        ident = R_const.alloc([128, 128], BF16)
        maskZ = R_const.alloc([128, 128], BF16)
        maskM = R_const.alloc([128, 128], F32)
        ones_f = R_const.alloc([128, 512], F32)
        g_pre_t = R_const.alloc([128, 8], F32)
        g_q_t = R_const.alloc([128, 2], F32)
        g_kv_t = R_const.alloc([128, 1], F32)
        g_o_t = R_const.alloc([128, 8], F32)
        freq_t = R_const.alloc([32, 1], F32, parts=32)
        ones_row = R_const.alloc([1, 128], BF16)
        neghalf = R_const.alloc([128, 16], F32)
        epsb = {}
        for cval in (D * EPS, 256 * EPS, 128 * EPS, 64 * EPS):
            epsb[cval] = R_const.alloc([128, 1], F32)
        sem_c = P.new_dma_sem("d_const")
        sem_w = [P.new_dma_sem("d_w0"), P.new_dma_sem("d_w1")]
        sem_x = [P.new_dma_sem("d_x0"), P.new_dma_sem("d_x1")]
        sem_p = [P.new_dma_sem("d_p0"), P.new_dma_sem("d_p1"), P.new_dma_sem("d_p2")]
        sem_o = [P.new_dma_sem("d_o0"), P.new_dma_sem("d_o1")]
        sem_pos = P.new_dma_sem("d_pos")
        sem_dbg = P.new_dma_sem("d_dbg")

        tmpf = R_tmp.alloc([128, 128], F32)
        MSET("pool", tmpf, 0.0, ["tmpf"])
        P.op("pool", lambda e: e.affine_select(out=tmpf, in_=tmpf, pattern=[[-1, 128]], compare_op=ALU.not_equal,
                                                fill=1.0, base=0, channel_multiplier=1), ["tmpf"], ["tmpf"])
        CP("dve", ident, tmpf, ["tmpf"], ["ident"])
        MSET("pool", maskM, 1.0, ["maskM"])
        P.op("pool", lambda e: e.affine_select(out=maskM, in_=maskM, pattern=[[-1, 128]], compare_op=ALU.is_ge,
                                                fill=0.0, base=0, channel_multiplier=1), ["maskM"], ["maskM"])
        TS("dve", maskZ, maskM, -1.0, 1.0, ALU.mult, ALU.add, ["maskM"], ["maskZ"])
        MSET("pool", ones_f, 1.0, ["ones_f"])
        MSET("pool", ones_row, 1.0, ["ones_row"])
        MSET("pool", neghalf, -0.5, ["neghalf"])
        MSET("pool", fsc["act"], 0.0, ["fsc"])
        for cval, tl in epsb.items():
            MSET("pool", tl, float(cval), ["epsb"])
        DMA("sp", sem_c, g_pre_t, g_pre_d.rearrange("(k p) -> p k", p=128), [], ["g_pre_t", "cser"], slow=True)
        DMA("sp", sem_c, g_q_t, g_q_d.rearrange("(k p) -> p k", p=128), [], ["g_q_t", "cser"], slow=True)
        DMA("sp", sem_c, g_kv_t, g_kv_d.rearrange("(k p) -> p k", p=128), [], ["g_kv_t", "cser"], slow=True)
        DMA("sp", sem_c, g_o_t[:, 0:4], g_sbo_d.rearrange("(k p) -> p k", p=128), [], ["g_o_t", "cser"], slow=True)
        DMA("sp", sem_c, g_o_t[:, 4:8], g_mlo_d.rearrange("(k p) -> p k", p=128), [], ["g_o_t", "cser"], slow=True)
        DMA("sp", sem_c, freq_t, freq_d, [], ["freq_t", "cser"])

        wslot = [0]

        def load_w(dst, src, ncols, scale, stage, tag):
            i = wslot[0] % 2
            wslot[0] += 1
            stg = stage[i][:, 0:ncols]
            DMA("sp", sem_w[i], stg, src, [], ["wst%d" % i])
            rd = ["wst%d" % i, "g_pre_t", "g_q_t", "g_kv_t", "g_o_t"]
            if i == 0:
                if scale is None:
                    CP("dve", dst, stg, rd, [tag])
                else:
                    TS("dve", dst, stg, scale, None, ALU.mult, None, rd, [tag])
            else:
                if scale is None:
                    CP("act", dst, stg, rd, [tag])
                else:
                    ACT(dst, stg, AF.Copy, rd, [tag], scale=scale)

        def norm_transpose_block(blk, xb, xs, junk, ssq, rq, hT, hT_tag, col0, tp_bank, xi, evac_eng):
            DMA("sp", sem_x[xi], xb, x_d[blk * 128:(blk + 1) * 128, :], [], ["xb%d" % xi])
            ACT(junk, xb, AF.Square, ["xb%d" % xi], ["ssq"], accum=ssq)
            RSQ(rq, ssq, D * EPS, 1, ["ssq"], ["rq"])
            TS("dve", xs, xb, rq[:, 0:1], 32.0, ALU.mult, ALU.mult, ["xb%d" % xi, "rq"], ["xs"])
            for kc in range(8):
                TR(pbf(tp_bank)[:, kc * 128:(kc + 1) * 128], xs[:, kc * 128:(kc + 1) * 128], ["xs"], ["ps%d" % tp_bank])
            CP(evac_eng, hT[:, :, col0:col0 + 128], pbf(tp_bank).rearrange("p (k t) -> p k t", t=128),
               ["ps%d" % tp_bank], [hT_tag])

        QT = R_big.alloc([128, 4, S], BF16)
        KT = R_big.alloc([128, 4, S], BF16)
        dV = R_big.alloc([128, NB, 512], BF16)
        R_o2.reset()
        Wqkv = R_o2.alloc([128, 8, 1536], BF16)
        hTc = [R_o2.alloc([128, 8, 513], BF16), R_o2.alloc([128, 8, 513], BF16)]
        dhT = R_o2.alloc([128, 8, 512], BF16)
        xbuf = [R_o2.alloc([128, 1024], F32), R_o2.alloc([128, 1024], F32)]
        xs1 = R_tmp.alloc([128, 1024], BF16)
        junk = R_tmp.alloc([128, 1024], BF16)
        ssq = R_tmp.alloc([128, 1], F32)
        rq = R_tmp.alloc([128, 1], F32)
        wst = [R_tmp.alloc([128, 1536], F32), R_tmp.alloc([128, 1536], F32)]

        def p1_weights():
            for kc in range(8):
                load_w(Wqkv[:, kc, :], w_in_d[kc * 128:(kc + 1) * 128, 0:1536], 1536, g_pre_t[:, kc:kc + 1], wst, "Wqkv")

        xcnt1 = [0]

        def p1_norm(ci):
            c = NCH - 1 - ci
            hT = hTc[ci % 2]
            htag = "hTc%d" % (ci % 2)
            for j in range(4):
                xi = xcnt1[0] % 2
                xcnt1[0] += 1
                norm_transpose_block(c * 4 + j, xbuf[xi], xs1, junk, ssq, rq, hT, htag, j * 128, 6 + (j % 2), xi,
                                     "act" if j % 2 else "dve")
            if ci == 0:
                MSET("pool", hT[:, :, 512:513], 0.0, [htag])
            else:
                CP("pool", hT[:, :, 512:513], hTc[(ci - 1) % 2][:, :, 0:1], ["hTc%d" % ((ci - 1) % 2)], [htag])

        def p1_mm(ci):
            c = NCH - 1 - ci
            hT = hTc[ci % 2]
            htag = "hTc%d" % (ci % 2)
            TT("pool", dhT, hT[:, :, 1:513], hT[:, :, 0:512], ALU.subtract, [htag], ["dhT"])
            n_ev = 0
            for which, dst, dtag in ((0, QT, "QT"), (1, KT, "KT")):
                for f in range(4):
                    bank = n_ev % 4
                    for kc in range(8):
                        MM(psb[bank][:, :], Wqkv[:, kc, which * 512 + f * 128: which * 512 + (f + 1) * 128],
                           hT[:, kc, 0:512], kc == 0, kc == 7, ["Wqkv", htag], ["ps%d" % bank])
                    CP("act" if n_ev % 2 else "dve", dst[:, f, c * 512:(c + 1) * 512], psb[bank][:, :],
                       ["ps%d" % bank], [dtag])
                    n_ev += 1
            for j in range(4):
                bank = n_ev % 4
                for kc in range(8):
                    MM(psb[bank][:, :], dhT[:, kc, j * 128:(j + 1) * 128], Wqkv[:, kc, 1024:1536],
                       kc == 0, kc == 7, ["Wqkv", "dhT"], ["ps%d" % bank])
                CP("act" if n_ev % 2 else "dve", dV[:, c * 4 + j, :], psb[bank][:, :], ["ps%d" % bank], ["dV"])
                n_ev += 1

        P.replay_interleaved([P.collect(p1_weights), P.collect(p1_norm, 0)])
        for ci in range(NCH):
            lists = [P.collect(p1_mm, ci)]
            if ci + 1 < NCH:
                lists.append(P.collect(p1_norm, ci + 1))
            P.replay_interleaved(lists)
        P.barrier()

        Osb = R_osb.alloc([128, NB, 512], BF16)
        R_tmp.reset()
        NU = 4
        u_sb = [R_tmp.alloc([128, 512], F32) for _ in range(NU)]
        Pb = [[R_tmp.alloc([128, 512], BF16) for _ in range(2)] for _ in range(8)]
        NPT = 4
        PTs = [R_tmp.alloc([128, 512], BF16) for _ in range(NPT)]
        QAB = [[R_tmp.alloc([128, 2, 128], BF16) for _ in range(4)] for _ in range(2)]
        z512 = R_tmp.alloc([1, 512], BF16)
        MSET("pool", z512, 0.0, ["z512"])
        negmask = R_tmp.alloc([128, 128], BF16)
        TS("dve", negmask, maskM, -30000.0, None, ALU.mult, None, ["maskM"], ["negmask"])
        for par in range(2):
            for g_ in range(4):
                MSET("pool", QAB[par][g_][64:128, 0, :], 0.0, ["QAB%d_%d" % (par, g_)])
                MSET("pool", QAB[par][g_][0:64, 1, :], 0.0, ["QAB%d_%d" % (par, g_)])
        items = []
        for qb in range(NB):
            nkb = NB - qb
            ntile = (nkb + 3) // 4
            for tau in range(ntile):
                kb0 = qb + 4 * tau
                nblk = min(4, NB - kb0)
                for h in range(8):
                    items.append((qb, tau, kb0, nblk, h, tau == ntile - 1))
        NI = len(items)
        LA = 3
        LF = 2

        def sb_stageA(n):
            qb, tau, kb0, nblk, h, last = items[n]
            g, half = h // 2, h % 2
            zb = n % 3
            w = nblk * 128
            rows = slice(64 * half, 64 * half + 64)
            qtile = QAB[qb % 2][g]
            qabtag = "QAB%d_%d" % (qb % 2, g)
            if tau == 0 and h == 0:
                for qn in ([0, 1] if qb == 0 else [qb + 1]):
                    if qn < NB:
                        for g2 in range(4):
                            qt2 = QAB[qn % 2][g2]
                            tg2 = "QAB%d_%d" % (qn % 2, g2)
                            CP("pool", qt2[0:64, 0, :], QT[0:64, g2, qn * 128:(qn + 1) * 128], ["QT"], [tg2])
                            CP("pool", qt2[64:128, 1, :], QT[64:128, g2, qn * 128:(qn + 1) * 128], ["QT"], [tg2])
            MM(psb[zb][:, 0:w], qtile[:, half, :], KT[:, g, kb0 * 128: kb0 * 128 + w],
               True, tau != 0, [qabtag, "KT"], ["ps%d" % zb])
            if tau == 0:
                MM(psb[zb][:, 0:128], ident, negmask, False, True, ["ident", "negmask"], ["ps%d" % zb])
            ub = n % NU
            ACT(u_sb[ub][:, 0:w], psb[zb][:, 0:w], AF.Sigmoid, ["ps%d" % zb], ["u%d" % ub], scale=-0.125)
            pbuf = Pb[h][tau % 2]
            ptag = "Pb%d_%d" % (h, tau % 2)
            if tau == 0:
                P.op("dve", lambda e: e.tensor_tensor_scan(out=pbuf[:, 0:w], data0=u_sb[ub][:, 0:w], data1=ones_f[:, 0:w],
                                                           initial=1.0, op0=ALU.mult, op1=ALU.mult),
                     ["u%d" % ub, "ones_f"], [ptag])
                TT("pool", pbuf[:, 0:128], pbuf[:, 0:128], maskZ, ALU.mult, [ptag, "maskZ"], [ptag])
            else:
                prev = Pb[h][(tau - 1) % 2]
                P.op("dve", lambda e: e.tensor_tensor_scan(out=pbuf[:, 0:w], data0=u_sb[ub][:, 0:w], data1=ones_f[:, 0:w],
                                                           initial=prev[:, 511:512], op0=ALU.mult, op1=ALU.mult),
                     ["u%d" % ub, "ones_f", "Pb%d_%d" % (h, (tau - 1) % 2)], [ptag])

        def sb_stageD(n):
            qb, tau, kb0, nblk, h, last = items[n]
            w = nblk * 128
            pbuf = Pb[h][tau % 2]
            ptag = "Pb%d_%d" % (h, tau % 2)
            tb = 3 + (n % 2)
            for k in range(nblk):
                TR(pbf(tb)[:, k * 128:(k + 1) * 128], pbuf[:, k * 128:(k + 1) * 128], [ptag], ["ps%d" % tb])
            pt = n % NPT
            CP("act", PTs[pt][:, 0:w], pbf(tb)[:, 0:w], ["ps%d" % tb], ["PT%d" % pt])

        def sb_stageF(n):
            qb, tau, kb0, nblk, h, last = items[n]
            pt = n % NPT
            ob = 5 + (qb % 2)
            if tau == 0 and h == 0:
                MM(psb[ob][:, :], z512[0:1, 0:128], z512[0:1, :], True, False, ["z512"], ["ps%d" % ob])
            for k in range(nblk):
                MM(psb[ob][:, h * 64:(h + 1) * 64], PTs[pt][:, k * 128:(k + 1) * 128],
                   dV[:, kb0 + k, h * 64:(h + 1) * 64], False, last and h == 7 and k == nblk - 1,
                   ["PT%d" % pt, "dV"], ["ps%d" % ob])
            if last and h == 7:
                CP("dve", Osb[:, qb, :], psb[ob][:, :], ["ps%d" % ob], ["Osb"])

        for step in range(NI + LA + LF):
            if step < NI:
                sb_stageA(step)
            if 0 <= step - LA < NI:
                sb_stageD(step - LA)
            if 0 <= step - LA - LF < NI:
                sb_stageF(step - LA - LF)
        P.barrier()

        R_big.reset()
        KnT = R_big.alloc([128, 4, S], BF16)
        Vaug = R_big.alloc([128, NB, 8, 65], BF16)
        cqnT = R_big.alloc([128, 2, S], BF16)
        KrT = R_big.alloc([128, S], BF16)
        R_oml.reset()
        R_tmp.reset()
        Wmla = R_oml.alloc([128, 8, 448], BF16)
        hT3r = [R_oml.alloc([128, 8, 512], BF16) for _ in range(2)]
        xbuf = [R_oml.alloc([128, 1024], F32), R_oml.alloc([128, 1024], F32)]
        ckvT = R_oml.alloc([128, 512], BF16)
        WuqN = R_tmp.alloc([128, 2, 512], BF16)
        WuqRA = R_tmp.alloc([128, 2, 256], BF16)
        WuqRB = R_tmp.alloc([128, 2, 256], BF16)
        posi = R_tmp.alloc([32, 512], I32, parts=32)
        ang = R_tmp.alloc([32, 512], F32, parts=32)
        ang2 = R_tmp.alloc([32, 512], F32, parts=32)
        cosT = R_tmp.alloc([32, 512], F32, parts=32)
        sinT = R_tmp.alloc([32, 512], F32, parts=32)
        rt1 = R_tmp.alloc([32, 512], F32, parts=32)
        rt2 = R_tmp.alloc([32, 512], F32, parts=32)
        negpi = R_tmp.alloc([32, 1], F32, parts=32)
        tmp_mark = R_tmp.ptr
        WukvK = R_tmp.alloc([128, 512], BF16)
        WukvV = R_tmp.alloc([128, 512], BF16)
        xs1 = R_tmp.alloc([128, 1024], BF16)
        junk = R_tmp.alloc([128, 1024], BF16)
        ssq = R_tmp.alloc([128, 1], F32)
        rq = R_tmp.alloc([128, 1], F32)
        ssq2a = R_tmp.alloc([128, 16], F32)
        ssq2b = R_tmp.alloc([128, 16], F32)
        rq2a = R_tmp.alloc([128, 16], F32)
        rq2b = R_tmp.alloc([128, 16], F32)
        cqs = R_tmp.alloc([128, 384], BF16)
        wst = [R_tmp.alloc([128, 1024], F32), R_tmp.alloc([128, 1024], F32)]

        def rope_tables(c, tag, tabs=None):
            sin_d, cos_d = tabs if tabs is not None else (sinT, cosT)
            DMA("sp", sem_pos, posi, pos_d[c * 512:(c + 1) * 512].partition_broadcast(32), [], ["posi"])
            CP("dve", ang, posi, ["posi"], ["ang"])
            TS("dve", ang, ang, freq_t[:, 0:1], None, ALU.mult, None, ["ang", "freq_t"], ["ang"])
            C1 = 6.28125
            C2 = float(2 * np.pi - 6.28125)
            for shift, dst, dtag in ((0.0, sin_d, "sinT" + tag), (float(np.pi / 2), cos_d, "cosT" + tag)):
                if shift:
                    TS("dve", ang2, ang, shift, None, ALU.add, None, ["ang"], ["ang2"])
                    src = ang2
                else:
                    src = ang
                TS("dve", rt1, src, float(1.0 / (2 * np.pi)), None, ALU.mult, None, ["ang", "ang2"], ["rt1"])
                CP("dve", posi, rt1, ["rt1"], ["posi"])
                CP("dve", rt1, posi, ["posi"], ["rt1"])
                STT("dve", rt2, rt1, -C1, src, ALU.mult, ALU.add, ["rt1", "ang", "ang2"], ["rt2"])
                STT("dve", rt2, rt1, -C2, rt2, ALU.mult, ALU.add, ["rt1", "rt2"], ["rt2"])
                TS("dve", rt2, rt2, float(np.pi), float(-np.pi), ALU.min, ALU.max, ["rt2"], ["rt2"])
                ACT(dst, rt2, AF.Sin, ["rt2"], [dtag])


        def p3_weights():
            for kc in range(8):
                i = wslot[0] % 2
                wslot[0] += 1
                stg = wst[i]
                DMA("sp", sem_w[i], stg[:, 0:416], w_in_d[kc * 128:(kc + 1) * 128, 2048:2464], [], ["wst%d" % i])
                sc = g_pre_t[:, kc:kc + 1]
                TS("dve", Wmla[:, kc, 0:416], stg[:, 0:416], sc, None, ALU.mult, None, ["wst%d" % i, "g_pre_t"], ["Wmla"])
                TS("dve", Wmla[:, kc, 416:432], stg[:, 400:416], sc, -1.0, ALU.mult, ALU.mult, ["wst%d" % i, "g_pre_t"], ["Wmla"])
                TS("dve", Wmla[:, kc, 432:448], stg[:, 384:400], sc, None, ALU.mult, None, ["wst%d" % i, "g_pre_t"], ["Wmla"])
            for kc in range(2):
                i = wslot[0] % 2
                wslot[0] += 1
                stg = wst[i]
                DMA("sp", sem_w[i], stg[:, 0:768], w_uq_d[kc * 128:(kc + 1) * 128, :], [], ["wst%d" % i])
                sc = g_q_t[:, kc:kc + 1]
                sv = stg[:, 0:768].rearrange("p (h c) -> p h c", c=96)
                rd, wr = ["wst%d" % i, "g_q_t"], ["Wuq"]
                TS("dve", WuqN[:, kc, :].rearrange("p (h c) -> p h c", c=64), sv[:, :, 0:64], sc, None, ALU.mult, None, rd, wr)
                TS("dve", WuqRA[:, kc, :].rearrange("p (h c) -> p h c", c=32), sv[:, :, 64:96], sc, None, ALU.mult, None, rd, wr)
                rb = WuqRB[:, kc, :].rearrange("p (h c) -> p h c", c=32)
                TS("dve", rb[:, :, 0:16], sv[:, :, 80:96], sc, -1.0, ALU.mult, ALU.mult, rd, wr)
                TS("dve", rb[:, :, 16:32], sv[:, :, 64:80], sc, None, ALU.mult, None, rd, wr)
            i = wslot[0] % 2
            wslot[0] += 1
            DMA("sp", sem_w[i], wst[i][:, 0:1024], w_ukv_d[:, :], [], ["wst%d" % i])
            sv = wst[i][:, 0:1024].rearrange("p (h c) -> p h c", c=128)
            TS("dve", WukvK.rearrange("p (h c) -> p h c", c=64), sv[:, :, 0:64], g_kv_t[:, 0:1], None, ALU.mult, None,
               ["wst%d" % i, "g_kv_t"], ["Wukv"])
            TS("dve", WukvV.rearrange("p (h c) -> p h c", c=64), sv[:, :, 64:128], g_kv_t[:, 0:1], None, ALU.mult, None,
               ["wst%d" % i, "g_kv_t"], ["Wukv"])
        MSET("pool", Vaug[:, :, :, 64:65], 1.0, ["Vaug"])
        MSET("pool", KrT[32:64, :], 0.0, ["KrT"])
        MSET("pool", KrT[64:128, :], 0.0, ["KrT"])

        xcnt3 = [0]

        def p3_norm(c):
            hT3 = hT3r[c % 2]
            for j in range(4):
                xi = xcnt3[0] % 2
                xcnt3[0] += 1
                norm_transpose_block(c * 4 + j, xbuf[xi], xs1, junk, ssq, rq, hT3, "hT3_%d" % (c % 2), j * 128, 6 + (j % 2), xi,
                                     "act" if j % 2 else "dve")

        def p3_mm(c):
            hT3 = hT3r[c % 2]
            h3tag = "hT3_%d" % (c % 2)
            rope_tables(c, "")
            for kc in range(8):
                MM(psb[4][0:32, :], Wmla[:, kc, 384:416], hT3[:, kc, :], kc == 0, kc == 7, ["Wmla", h3tag], ["ps4"])
            for kc in range(8):
                MM(psb[5][0:32, :], Wmla[:, kc, 416:448], hT3[:, kc, :], kc == 0, kc == 7, ["Wmla", h3tag], ["ps5"])
            TT("dve", rt1, psb[4][0:32, :], cosT, ALU.mult, ["ps4", "cosT"], ["rt1"])
            TT("dve", rt2, psb[5][0:32, :], sinT, ALU.mult, ["ps5", "sinT"], ["rt2"])
            TT("dve", KrT[0:32, c * 512:(c + 1) * 512], rt1, rt2, ALU.add, ["rt1", "rt2"], ["KrT"])
            for j in range(4):
                bank = j % 2
                for kc in range(8):
                    MM(psb[bank][:, 0:384], hT3[:, kc, j * 128:(j + 1) * 128], Wmla[:, kc, 0:384], kc == 0, kc == 7,
                       ["Wmla", h3tag], ["ps%d" % bank])
                ACT(junk[:, 0:256], psb[bank][:, 0:256], AF.Square, ["ps%d" % bank], ["ssq2a"], accum=ssq2a[:, 0:1])
                ACT(junk[:, 256:384], psb[bank][:, 256:384], AF.Square, ["ps%d" % bank], ["ssq2b"], accum=ssq2b[:, 0:1])
                RSQ(rq2a[:, 0:1], ssq2a[:, 0:1], 256 * EPS, 1, ["ssq2a"], ["rq2a"])
                RSQ(rq2b[:, 0:1], ssq2b[:, 0:1], 128 * EPS, 1, ["ssq2b"], ["rq2b"])
                TS("dve", cqs[:, 0:256], psb[bank][:, 0:256], rq2a[:, 0:1], 16.0, ALU.mult, ALU.mult,
                   ["ps%d" % bank, "rq2a"], ["cqs"])
                TS("dve", cqs[:, 256:384], psb[bank][:, 256:384], rq2b[:, 0:1], float(np.sqrt(128.0)), ALU.mult, ALU.mult,
                   ["ps%d" % bank, "rq2b"], ["cqs"])
                tb = 2 + (j % 2)
                for k3 in range(3):
                    TR(pbf(tb)[:, k3 * 128:(k3 + 1) * 128], cqs[:, k3 * 128:(k3 + 1) * 128], ["cqs"], ["ps%d" % tb])
                tok = c * 512 + j * 128
                CP("act", cqnT[:, :, tok:tok + 128], pbf(tb)[:, 0:256].rearrange("p (k t) -> p k t", t=128),
                   ["ps%d" % tb], ["cqnT"])
                CP("act", ckvT[:, j * 128:(j + 1) * 128], pbf(tb)[:, 256:384], ["ps%d" % tb], ["ckvT"])
            for g in range(4):
                bank = g % 2
                MM(psb[bank][:, :], WukvK[:, g * 128:(g + 1) * 128], ckvT[:, :], True, True, ["Wukv", "ckvT"], ["ps%d" % bank])
                CP("act" if g % 2 else "dve", KnT[:, g, c * 512:(c + 1) * 512], psb[bank][:, :], ["ps%d" % bank], ["KnT"])
            for j in range(4):
                bank = 4 + (j % 2)
                MM(psb[bank][:, :], ckvT[:, j * 128:(j + 1) * 128], WukvV[:, :], True, True, ["Wukv", "ckvT"], ["ps%d" % bank])
                CP("act" if j % 2 else "dve", Vaug[:, c * 4 + j, :, 0:64],
                   psb[bank][:, :].rearrange("p (h c) -> p h c", c=64), ["ps%d" % bank], ["Vaug"])

        P.replay_interleaved([P.collect(p3_weights), P.collect(p3_norm, 0)])
        for c in range(NCH):
            lists = [P.collect(p3_mm, c)]
            if c + 1 < NCH:
                lists.append(P.collect(p3_norm, c + 1))
            if SEQ3:
                for l in lists:
                    P.replay_interleaved([l])
            else:
                P.replay_interleaved(lists)
        P.barrier()

        R_oml.reset()
        Oml = R_oml.alloc([128, NB, 512], BF16)
        R_tmp.ptr = tmp_mark
        QnT = [R_tmp.alloc([128, 8, 512], BF16)] * 2
        QrT = [R_tmp.alloc([128, 8, 512], BF16)] * 2
        for hh in range(8):
            if hh % 2 == 0:
                MSET("pool", QnT[0][64:128, hh, :], 0.0, ["QnT%d" % hh])
            else:
                MSET("pool", QnT[0][0:64, hh, :], 0.0, ["QnT%d" % hh])
        MSET("pool", QrT[0][32:64, :, :], 0.0, ["QrT%d" % hh for hh in range(8)])
        MSET("pool", QrT[0][64:128, :, :], 0.0, ["QrT%d" % hh for hh in range(8)])
        NAT = 5
        ATs = [R_big.alloc([128, 512], BF16) for _ in range(NAT)]
        rec = R_tmp.alloc([128, 4], F32)
        z512b = R_tmp.alloc([1, 512], BF16)
        MSET("pool", z512b, 0.0, ["z512b"])
        SCALE = float(96 ** -0.5)
        an = [0]
        sinT2 = R_tmp.alloc([32, 512], F32, parts=32)
        cosT2 = R_tmp.alloc([32, 512], F32, parts=32)
        cs = [(sinT, cosT), (sinT2, cosT2)]

        def q_tables(qc):
            rope_tables(qc, "q%d" % (qc % 2), tabs=cs[qc % 2])

        def qn_proj(qc, g):
            for kc in range(2):
                MM(psb[5][:, :], WuqN[:, kc, g * 128:(g + 1) * 128], cqnT[:, kc, qc * 512:(qc + 1) * 512],
                   kc == 0, kc == 1, ["Wuq", "cqnT"], ["ps5"])
            CP("dve", QnT[0][0:64, 2 * g, :], psb[5][0:64, :], ["ps5"], ["QnT%d" % (2 * g)])
            CP("dve", QnT[0][64:128, 2 * g + 1, :], psb[5][64:128, :], ["ps5"], ["QnT%d" % (2 * g + 1)])

        def rope_head(qc, h):
            sin_d, cos_d = cs[qc % 2]
            sfx = "q%d" % (qc % 2)
            for kc in range(2):
                MM(psb[6][0:32, :], WuqRA[:, kc, h * 32:(h + 1) * 32], cqnT[:, kc, qc * 512:(qc + 1) * 512],
                   kc == 0, kc == 1, ["Wuq", "cqnT"], ["ps6"])
            for kc in range(2):
                MM(psb[7][0:32, :], WuqRB[:, kc, h * 32:(h + 1) * 32], cqnT[:, kc, qc * 512:(qc + 1) * 512],
                   kc == 0, kc == 1, ["Wuq", "cqnT"], ["ps7"])
            TT("dve", rt1, psb[6][0:32, :], cos_d, ALU.mult, ["ps6", "cosT" + sfx], ["rt1"])
            TT("dve", rt2, psb[7][0:32, :], sin_d, ALU.mult, ["ps7", "sinT" + sfx], ["rt2"])
            TT("dve", QrT[0][0:32, h, :], rt1, rt2, ALU.add, ["rt1", "rt2"], ["QrT%d" % h])

        q_tables(0)
        for g in range(4):
            qn_proj(0, g)
        for h in range(8):
            rope_head(0, h)

        for qc in range(NCH):
            qi = 0
            if qc + 1 < NCH:
                q_tables(qc + 1)

            mitems = []
            for h in range(8):
                for kb in range(4 * qc, NB):
                    mitems.append((h, kb))
            LM = 4

            def ml_stageA(m):
                h, kb = mitems[m]
                g, half = h // 2, h % 2
                rows = slice(64 * half, 64 * half + 64)
                i = kb - 4 * qc
                ncols = 512 if i >= 4 else (i + 1) * 128
                n = an[0] + m
                zb = n % 3
                MM(psb[zb][:, 0:ncols], KnT[:, g, kb * 128:(kb + 1) * 128], QnT[qi][:, h, 0:ncols], True, False,
                   ["KnT", "QnT%d" % h], ["ps%d" % zb])
                MM(psb[zb][:, 0:ncols], KrT[:, kb * 128:(kb + 1) * 128], QrT[qi][:, h, 0:ncols], False, True,
                   ["KrT", "QrT%d" % h], ["ps%d" % zb])
                at = n % NAT
                A = ATs[at]
                atag = "AT%d" % at
                if i >= 4:
                    ACT(A[:, :], psb[zb][:, :], AF.Exp, ["ps%d" % zb], [atag], scale=SCALE)
                else:
                    c0 = i * 128
                    ACT(A[:, 0:c0 + 128], psb[zb][:, 0:c0 + 128], AF.Exp, ["ps%d" % zb], [atag], scale=SCALE)
                    MSET("pool", A[0:64, c0 + 64:c0 + 128], 0.0, [atag])

            def ml_stageD(m):
                h, kb = mitems[m]
                i = kb - 4 * qc
                nj = 4 if i >= 4 else i + 1
                n = an[0] + m
                at = n % NAT
                ob = 3 + (h % 2)
                oacc = psb[ob][:, 0:260].rearrange("p (j c) -> p j c", c=65)
                if kb == 4 * qc:
                    MM(psb[ob][:, 0:260], z512b[0:1, 0:128], z512b[0:1, 0:260], True, False, ["z512b"], ["ps%d" % ob])
                for j in range(nj):
                    MM(oacc[:, j, :], ATs[at][:, j * 128:(j + 1) * 128], Vaug[:, kb, h, :], False, kb == NB - 1 and j == nj - 1,
                       ["AT%d" % at, "Vaug"], ["ps%d" % ob])
                if kb == NB - 1:
                    P.op("dve", lambda e: e.reciprocal(out=rec[:, :].unsqueeze(2), in_=oacc[:, :, 64:65]),
                         ["ps%d" % ob], ["rec"], fence=True)
                    for j in range(4):
                        TS("dve", Oml[:, qc * 4 + j, h * 64:(h + 1) * 64], oacc[:, j, 0:64], rec[:, j:j + 1], None,
                           ALU.mult, None, ["ps%d" % ob, "rec"], ["Oml"])

            NM = len(mitems)
            for step in range(NM + LM):
                if step < NM:
                    ml_stageA(step)
                if 0 <= step - LM < NM:
                    ml_stageD(step - LM)
                    hd, kbd = mitems[step - LM]
                    if kbd == NB - 1 and qc + 1 < NCH:
                        rope_head(qc + 1, hd)
                        if hd % 2 == 1:
                            qn_proj(qc + 1, hd // 2)
            an[0] += NM
        P.barrier()

        R_big.reset()
        R_tmp.reset()
        Wg = R_big.alloc([128, 8, 1024], BF16)
        Wv5 = R_big.alloc([128, 8, 512], BF16)
        Wout = R_big.alloc([128, 8, 1024], BF16)
        Wpg = R_big.alloc([128, 8, 1024], BF16)
        Wple = R_big.alloc([128, 2, 1024], BF16)
        gpost_b = R_big.alloc([128, 1024], F32)
        gple_b = R_big.alloc([128, 1024], F32)
        bpg_b = R_big.alloc([1, 1024], BF16)
        hTb = [R_big.alloc([128, 8, 129], BF16) for _ in range(3)]
        yT = R_big.alloc([128, 8, 128], BF16)
        x1T = R_big.alloc([128, 8, 128], BF16)
        pT = R_big.alloc([128, 2, 128], BF16)
        pbb = R_big.alloc([128, 256], BF16)
        ybr = [R_big.alloc([128, 1024], BF16) for _ in range(2)]
        x1b = R_big.alloc([128, 1024], BF16)
        pbuf = [R_big.alloc([128, 256], F32) for _ in range(4)]
        tmp5 = R_tmp.ptr
        wst = [R_tmp.alloc([128, 1024], F32), R_tmp.alloc([128, 1024], F32)]
        bpg_f = R_tmp.alloc([1, 1024], F32)
        small5 = R_big.alloc([128, 16 * 12], F32)
        ssq = small5[:, 0:1]
        rq = small5[:, 16:17]
        s16 = small5[:, 32:48]
        r16 = small5[:, 48:64]
        sA0 = small5[:, 64:65]
        sA1 = small5[:, 80:81]
        sB = small5[:, 96:97]
        rB = small5[:, 112:113]
        sC0 = small5[:, 128:129]
        sC1 = small5[:, 144:145]
        sD = small5[:, 160:161]
        rD = small5[:, 176:177]

        for kc in range(8):
            i = wslot[0] % 2
            wslot[0] += 1
            DMA("sp", sem_w[i], wst[i][:, 0:512], w_in_d[kc * 128:(kc + 1) * 128, 1536:2048], [], ["wst%d" % i])
            DMA("sp", sem_w[i], wst[i][:, 512:1024], w_in_d[kc * 128:(kc + 1) * 128, 2464:2976], [], ["wst%d" % i])
            if i == 0:
                TS("dve", Wg[:, kc, :], wst[i][:, :], g_pre_t[:, kc:kc + 1], None, ALU.mult, None, ["wst%d" % i, "g_pre_t"], ["Wg"])
            else:
                ACT(Wg[:, kc, :], wst[i][:, :], AF.Copy, ["wst%d" % i, "g_pre_t"], ["Wg"], scale=g_pre_t[:, kc:kc + 1])
        for kc in range(8):
            load_w(Wv5[:, kc, :], w_in_d[kc * 128:(kc + 1) * 128, 1024:1536], 512, g_pre_t[:, kc:kc + 1], wst, "Wv5")
        for kc in range(8):
            load_w(Wout[:, kc, :], w_out_d[kc * 128:(kc + 1) * 128, :], 1024, g_o_t[:, kc:kc + 1], wst, "Wout")
        for kc in range(8):
            load_w(Wpg[:, kc, :], w_pg_d[kc * 128:(kc + 1) * 128, :], 1024, None, wst, "Wpg")
        for kc in range(2):
            load_w(Wple[:, kc, :], w_ple_d[kc * 128:(kc + 1) * 128, :], 1024, None, wst, "Wple")
        DMA("sp", sem_c, gpost_b, g_post_d.partition_broadcast(128), [], ["gpost_b", "cser"])
        DMA("sp", sem_c, gple_b, g_ple_d.partition_broadcast(128), [], ["gple_b", "cser"])
        DMA("sp", sem_c, bpg_f, b_pg_d.partition_broadcast(1), [], ["bpg_f", "cser"])
        TS("dve", gpost_b, gpost_b, 32.0, None, ALU.mult, None, ["gpost_b"], ["gpost_b"])
        TS("dve", gple_b, gple_b, 32.0, None, ALU.mult, None, ["gple_b"], ["gple_b"])
        CP("dve", bpg_b, bpg_f, ["bpg_f"], ["bpg_b"])
        P.barrier()
        R_tmp.ptr = tmp5
        xbuf = [R_tmp.alloc([128, 1024], F32) for _ in range(3)]
        x1r = [R_tmp.alloc([128, 1024], F32) for _ in range(2)]
        plr = [R_tmp.alloc([128, 1024], F32)] * 2
        sg = R_tmp.alloc([128, 1024], F32)
        ofp = R_tmp.alloc([128, 1024], F32)
        sqb = R_tmp.alloc([128, 1024], F32)
        sg2 = R_tmp.alloc([128, 1024], F32)
        xs1 = R_big.alloc([128, 1024], BF16)
        sem_x3 = [sem_x[0], sem_x[1], sem_pos]
        sem_p4 = [sem_p[0], sem_p[1], sem_p[2], sem_w[0]]

        def st_A(bi):
            tb = NB - 1 - bi
            xi = bi % 3
            xb = xbuf[xi]
            hT = hTb[bi % 3]
            htag = "hTb%d" % (bi % 3)
            DMA("sp", sem_p4[bi % 4], pbuf[bi % 4], p_d[tb * 128:(tb + 1) * 128, :], [], ["pb%d" % (bi % 4)])
            DMA("sp", sem_x3[xi], xb, x_d[tb * 128:(tb + 1) * 128, :], [], ["xb%d" % xi])
            ACT(xs1, xb, AF.Square, ["xb%d" % xi], ["ssq", "xs"], accum=ssq)
            RSQ(rq, ssq, D * EPS, 1, ["ssq"], ["rq"])
            TS("dve", xs1, xb, rq[:, 0:1], 32.0, ALU.mult, ALU.mult, ["xb%d" % xi, "rq"], ["xs"])
            P.group_begin()
            for kc in range(8):
                TR(pbf(0)[:, kc * 128:(kc + 1) * 128], xs1[:, kc * 128:(kc + 1) * 128], ["xs"], ["ps0"])
            CP("dve", hT[:, :, 0:128], pbf(0).rearrange("p (k t) -> p k t", t=128), ["ps0"], [htag])
            P.group_end()
            if bi == 0:
                MSET("pool", hT[:, :, 128:129], 0.0, [htag])
            else:
                CP("pool", hT[:, :, 128:129], hTb[(bi - 1) % 3][:, :, 0:1], ["hTb%d" % ((bi - 1) % 3)], [htag])

        def st_B1(bi):
            tb = NB - 1 - bi
            hT = hTb[bi % 3]
            htag = "hTb%d" % (bi % 3)
            yb = ybr[bi % 2]
            ytag = "yb%d" % (bi % 2)
            for half in range(2):
                for kc in range(8):
                    MM(psb[1 + half][:, :], hT[:, kc, 0:128], Wg[:, kc, half * 512:(half + 1) * 512], kc == 0, kc == 7,
                       [htag, "Wg"], ["ps%d" % (1 + half)])
            for kc in range(8):
                MM(psb[3][:, :], hT[:, kc, 1:129], Wv5[:, kc, :], kc == 0, kc == 7, [htag, "Wv5"], ["ps3"])
            ACT(sg[:, 0:512], psb[1][:, :], AF.Silu, ["ps1"], ["sg"])
            ACT(sg[:, 512:1024], psb[2][:, :], AF.Silu, ["ps2"], ["sg"])
            TT("dve", ofp[:, 0:512], psb[3][:, :], Osb[:, tb, :], ALU.add, ["ps3", "Osb"], ["ofp"])
            CP("act", ofp[:, 512:1024], Oml[:, tb, :], ["Oml"], ["ofp"])
            if DBG:
                DMA("sp", sem_dbg, dbg_sb_d[tb * 128:(tb + 1) * 128, :], ofp[:, 0:512], ["ofp"], [])
                DMA("sp", sem_dbg, dbg_ml_d[tb * 128:(tb + 1) * 128, :], ofp[:, 512:1024], ["ofp"], [])
            TT("dve", sqb, ofp, ofp, ALU.mult, ["ofp"], ["sqb"])
            P.op("dve", lambda e: e.tensor_reduce(out=s16, in_=sqb.rearrange("p (h c) -> p h c", c=64), axis=AX.X,
                                                  op=ALU.add), ["sqb"], ["s16"])
            RSQ(r16, s16, 64 * EPS, 16, ["s16"], ["r16"])
            TT("dve", sqb.rearrange("p (h c) -> p h c", c=64), ofp.rearrange("p (h c) -> p h c", c=64),
               r16.unsqueeze(2).broadcast_to([128, 16, 64]), ALU.mult, ["ofp", "r16"], ["sqb"])
            STT("dve", yb, sqb, 8.0, sg, ALU.mult, ALU.mult, ["sqb", "sg"], [ytag])

        def st_B2(bi):
            xi = bi % 3
            xb = xbuf[xi]
            yb = ybr[bi % 2]
            ytag = "yb%d" % (bi % 2)
            x1 = x1r[bi % 2]
            x1tag = "x1_%d" % (bi % 2)
            tb = NB - 1 - bi
            P.group_begin()
            for kc in range(8):
                TR(pbf(0)[:, kc * 128:(kc + 1) * 128], yb[:, kc * 128:(kc + 1) * 128], [ytag], ["ps0"])
            CP("act", yT, pbf(0).rearrange("p (k t) -> p k t", t=128), ["ps0"], ["yT"])
            P.group_end()
            for half in range(2):
                for kc in range(8):
                    MM(psb[4 + half][:, :], yT[:, kc, :], Wout[:, kc, half * 512:(half + 1) * 512], kc == 0, kc == 7,
                       ["yT", "Wout"], ["ps%d" % (4 + half)])
            ACT(yb[:, 0:512], psb[4][:, :], AF.Square, ["ps4"], ["sA0", ytag], accum=sA0)
            ACT(yb[:, 512:1024], psb[5][:, :], AF.Square, ["ps5"], ["sA1", ytag], accum=sA1)
            RSQ2(rB, sA0, sA1, D * EPS, ["sA0", "sA1"], ["rB"])
            for half in range(2):
                STT("dve", x1[:, half * 512:(half + 1) * 512], psb[4 + half][:, :], rB[:, 0:1],
                    gpost_b[:, half * 512:(half + 1) * 512], ALU.mult, ALU.mult, ["ps%d" % (4 + half), "rB", "gpost_b"], [x1tag])
            TT("dve", x1, x1, xb, ALU.add, [x1tag, "xb%d" % xi], [x1tag])
            if DBG:
                DMA("sp", sem_dbg, dbg_x1_d[tb * 128:(tb + 1) * 128, :], x1, [x1tag], [])

        def st_C(bi):
            tb = NB - 1 - bi
            x1 = x1r[bi % 2]
            x1tag = "x1_%d" % (bi % 2)
            pl = plr[0]
            pltag = "pl"
            pb_ = pbuf[bi % 4]
            CP("act", x1b, x1, [x1tag], ["x1b"])
            CP("act", pbb, pb_, ["pb%d" % (bi % 4)], ["pbb"])
            P.group_begin()
            for kc in range(8):
                TR(pbf(0)[:, kc * 128:(kc + 1) * 128], x1b[:, kc * 128:(kc + 1) * 128], ["x1b"], ["ps0"])
            CP("act", x1T, pbf(0).rearrange("p (k t) -> p k t", t=128), ["ps0"], ["x1T"])
            P.group_end()
            P.group_begin()
            for kc in range(2):
                TR(pbf(0)[:, kc * 128:(kc + 1) * 128], pbb[:, kc * 128:(kc + 1) * 128], ["pbb"], ["ps0"])
            CP("dve", pT, pbf(0)[:, 0:256].rearrange("p (k t) -> p k t", t=128), ["ps0"], ["pT"])
            P.group_end()
            for half in range(2):
                for kc in range(8):
                    MM(psb[6 + half][:, :], x1T[:, kc, :], Wpg[:, kc, half * 512:(half + 1) * 512], kc == 0, False,
                       ["x1T", "Wpg"], ["ps%d" % (6 + half)])
                MM(psb[6 + half][:, :], ones_row[0:1, :], bpg_b[0:1, half * 512:(half + 1) * 512], False, True,
                   ["ones_row", "bpg_b"], ["ps%d" % (6 + half)])
            ACT(sg2[:, 0:512], psb[6][:, :], AF.Sigmoid, ["ps6"], ["sg2"])
            ACT(sg2[:, 512:1024], psb[7][:, :], AF.Sigmoid, ["ps7"], ["sg2"])
            for half in range(2):
                for kc in range(2):
                    MM(psb[6 + half][:, :], pT[:, kc, :], Wple[:, kc, half * 512:(half + 1) * 512], kc == 0, kc == 1,
                       ["pT", "Wple"], ["ps%d" % (6 + half)])
            ACT(x1b[:, 0:512], psb[6][:, :], AF.Square, ["ps6"], ["sC0", "x1b"], accum=sC0)
            ACT(x1b[:, 512:1024], psb[7][:, :], AF.Square, ["ps7"], ["sC1", "x1b"], accum=sC1)
            RSQ2(rD, sC0, sC1, D * EPS, ["sC0", "sC1"], ["rD"])
            for half in range(2):
                STT("dve", pl[:, half * 512:(half + 1) * 512], psb[6 + half][:, :], rD[:, 0:1],
                    gple_b[:, half * 512:(half + 1) * 512], ALU.mult, ALU.mult, ["ps%d" % (6 + half), "rD", "gple_b"], [pltag])
            if DBG:
                DMA("sp", sem_dbg, dbg_pl_d[tb * 128:(tb + 1) * 128, :], pl, [pltag], [])
                DMA("sp", sem_dbg, dbg_sg_d[tb * 128:(tb + 1) * 128, :], sg2, ["sg2"], [])
            TT("dve", pl, pl, sg2, ALU.mult, [pltag, "sg2"], [pltag])
            TT("dve", pl, pl, x1, ALU.add, [pltag, x1tag], [pltag])
            DMA("sp", sem_o[0], out_d[tb * 128:(tb + 1) * 128, :], pl, [pltag], [])

        stages = [st_A, st_B1, st_B2, st_C]
        for step in range(NB + len(stages) - 1):
            lists = []
            for k in range(len(stages) - 1, -1, -1):
                if 0 <= step - k < NB:
                    lists.append(P.collect(stages[k], step - k))
            if SEQ5:
                for l in lists:
                    P.replay_interleaved([l])
            else:
                P.replay_interleaved(lists)
        P.wait_all("sp", ["pl"])
        if DBG:
            P.wait_all("sp", ["ofp"])
        P.finish()
    return nc


_NC_CACHE = {}


def kernel(x, p, positions, norm_pre_g, w_in, q_norm_g, w_uq, kv_norm_g, w_ukv, sb_out_norm_g, mla_out_norm_g,
           w_out, norm_post_g, w_ple, ple_norm_g, w_ple_gate, b_ple_gate):
    n = 8
    x = np.asarray(x, dtype=np.float32)
    p = np.asarray(p, dtype=np.float32)
    positions = np.asarray(positions, dtype=np.int32)
    half = 16
    freq = (np.float32(10000.0) ** (-(np.arange(half, dtype=np.float32)) / np.float32(half))).astype(np.float32)
    freq32 = np.concatenate([freq, freq]).reshape(32, 1).astype(np.float32)
    shared = {
        "freq": freq32,
        "norm_pre_g": np.ascontiguousarray(np.asarray(norm_pre_g, np.float32)[0]),
        "w_in": np.ascontiguousarray(np.asarray(w_in, np.float32)[0]),
        "q_norm_g": np.ascontiguousarray(np.asarray(q_norm_g, np.float32)[0]),
        "w_uq": np.ascontiguousarray(np.asarray(w_uq, np.float32)[0]),
        "kv_norm_g": np.ascontiguousarray(np.asarray(kv_norm_g, np.float32)[0]),
        "w_ukv": np.ascontiguousarray(np.asarray(w_ukv, np.float32)[0]),
        "sb_out_norm_g": np.ascontiguousarray(np.asarray(sb_out_norm_g, np.float32)[0]),
        "mla_out_norm_g": np.ascontiguousarray(np.asarray(mla_out_norm_g, np.float32)[0]),
        "w_out": np.ascontiguousarray(np.asarray(w_out, np.float32)[0]),
        "norm_post_g": np.ascontiguousarray(np.asarray(norm_post_g, np.float32)[0]),
        "w_ple": np.ascontiguousarray(np.asarray(w_ple, np.float32)[0]),
        "ple_norm_g": np.ascontiguousarray(np.asarray(ple_norm_g, np.float32)[0]),
        "w_ple_gate": np.ascontiguousarray(np.asarray(w_ple_gate, np.float32)[0]),
        "b_ple_gate": np.ascontiguousarray(np.asarray(b_ple_gate, np.float32)[0]),
    }
    in_maps = []
    for b in range(n):
        m = dict(shared)
        m["x"] = np.ascontiguousarray(x[b, ::-1, :])
        m["p"] = np.ascontiguousarray(p[0, b, ::-1, :])
        m["pos"] = np.ascontiguousarray(positions[b, ::-1])
        in_maps.append(m)
    if "nc" not in _NC_CACHE:
        _NC_CACHE["nc"] = build()
    nc = _NC_CACHE["nc"]
    res = run_bass_kernel_spmd(nc, in_maps, core_ids=list(range(n)))
    out = np.stack([np.asarray(r["out"])[::-1, :] for r in res.results], axis=0)
    kernel.last_results = res.results
    return np.ascontiguousarray(out.astype(np.float32))
```

```python
import numpy as np
from contextlib import ExitStack
import concourse.bass as bass
import concourse.mybir as mybir
from concourse.bass_utils import run_bass_kernel_spmd

F32 = mybir.dt.float32
BF16 = mybir.dt.bfloat16
I32 = mybir.dt.int32
AF = mybir.ActivationFunctionType
ALU = mybir.AluOpType
AX = mybir.AxisListType

S = 4096
D = 1024
NB = S // 128
NCH = S // 512
EPS = 1e-6
DBG = False
SEQ5 = False
SEQ3 = False


class Prog:
    ENG = ("pe", "act", "dve", "pool", "sp")

    def __init__(self, nc, stack):
        self.nc = nc
        self.stack = stack
        self.lists = {e: [] for e in self.ENG}
        self.sem = {e: stack.enter_context(nc.semaphore("s_" + e)) for e in self.ENG}
        self.cnt = {e: 0 for e in self.ENG}
        self.waited = {e: {} for e in self.ENG}
        self.lastw = {}
        self.reads = {}
        self.semobj = {e: self.sem[e] for e in self.ENG}
        self.dma_cnt = {}
        self.capture = None
        self.fence_fn = {}
        self.cur_gid = None
        self.gid_ctr = 0

    def new_dma_sem(self, name):
        s = self.stack.enter_context(self.nc.semaphore(name))
        self.semobj[name] = s
        self.dma_cnt[name] = 0
        return name

    def _deps(self, reads, writes):
        deps = {}

        def add(k, v):
            if deps.get(k, 0) < v:
                deps[k] = v
        for r in reads:
            if r in self.lastw:
                add(*self.lastw[r])
        for w in writes:
            if w in self.lastw:
                add(*self.lastw[w])
            for k, v in self.reads.get(w, {}).items():
                add(k, v)
        return deps

    def _emit_waits(self, eng, deps):
        for k, v in deps.items():
            if k == eng and eng == "pe":
                continue
            if self.waited[eng].get(k, 0) >= v:
                continue
            self.waited[eng][k] = v
            so = self.semobj[k]
            self.lists[eng].append(lambda e, so=so, v=v: e.wait_ge(so, v))

    def _record(self, key, val, reads, writes):
        for r in reads:
            d = self.reads.setdefault(r, {})
            if d.get(key, 0) < val:
                d[key] = val
        for w in writes:
            self.lastw[w] = (key, val)
            self.reads[w] = {}

    def op(self, eng, fn, reads=(), writes=(), fence=False):
        if self.capture is not None:
            self.capture.append((self.cur_gid, "op", eng, fn, list(reads), list(writes), fence))
            return
        self._emit_waits(eng, self._deps(reads, writes))
        self.cnt[eng] += 1
        so = self.sem[eng]
        self.lists[eng].append(lambda e, fn=fn, so=so: fn(e).then_inc(so, 1))
        if fence and self.fence_fn.get(eng) is not None:
            self.cnt[eng] += 1
            ff = self.fence_fn[eng]
            self.lists[eng].append(lambda e, ff=ff, so=so: ff(e).then_inc(so, 1))
        self._record(eng, self.cnt[eng], reads, writes)

    def dma(self, eng, semname, fn, reads=(), writes=()):
        if self.capture is not None:
            self.capture.append((self.cur_gid, "dma", eng, semname, fn, list(reads), list(writes)))
            return
        self._emit_waits(eng, self._deps(reads, writes))
        self.dma_cnt[semname] += 16
        so = self.semobj[semname]
        self.lists[eng].append(lambda e, fn=fn, so=so: fn(e).then_inc(so, 16))
        self._record(semname, self.dma_cnt[semname], reads, writes)

    def collect(self, fn, *args):
        self.capture = []
        fn(*args)
        ops, self.capture = self.capture, None
        return ops

    def group_begin(self):
        self.gid_ctr += 1
        self.cur_gid = self.gid_ctr

    def group_end(self):
        self.cur_gid = None

    def _emit_item(self, it):
        if it[1] == "op":
            self.op(*it[2:])
        else:
            self.dma(*it[2:])

    def replay_interleaved(self, lists):
        lists = [l for l in lists if l]
        idx = [0] * len(lists)
        total = max(len(l) for l in lists) if lists else 0
        for step in range(total):
            for li, l in enumerate(lists):
                hi = (step + 1) * len(l) // total
                while idx[li] < hi:
                    it = l[idx[li]]
                    idx[li] += 1
                    self._emit_item(it)
                    if it[0] is not None:
                        while idx[li] < len(l) and l[idx[li]][0] == it[0]:
                            self._emit_item(l[idx[li]])
                            idx[li] += 1

    def wait_all(self, eng, res):
        self._emit_waits(eng, self._deps(res, res))

    def barrier(self):
        allv = {e: self.cnt[e] for e in self.ENG if self.cnt[e] > 0}
        for k, v in self.dma_cnt.items():
            if v > 0:
                allv[k] = v
        for e in self.ENG:
            self._emit_waits(e, {k: v for k, v in allv.items() if k != e})

    def finish(self):
        with self.nc.Block() as block:
            @block.tensor
            def _(e):
                for f in self.lists["pe"]:
                    f(e)

            @block.scalar
            def _(e):
                for f in self.lists["act"]:
                    f(e)

            @block.vector
            def _(e):
                for f in self.lists["dve"]:
                    f(e)

            @block.gpsimd
            def _(e):
                for f in self.lists["pool"]:
                    f(e)

            @block.sync
            def _(e):
                for f in self.lists["sp"]:
                    f(e)


class Region:
    def __init__(self, arena, start, end):
        self.arena, self.start, self.end, self.ptr = arena, start, end, start

    def reset(self):
        self.ptr = self.start

    def alloc(self, shape, dt, parts=128):
        esz = 4 if dt in (F32, I32) else 2
        n = 1
        for s_ in shape[1:]:
            n *= s_
        nbytes = (n * esz + 31) // 32 * 32
        off = self.ptr
        assert off + nbytes <= self.end, ("region overflow", shape, off, nbytes, self.end)
        self.ptr += nbytes
        v = self.arena[0:shape[0], off // 2: off // 2 + (n * esz) // 2]
        if esz == 4:
            v = v.bitcast(dt)
        if len(shape) == 3:
            v = v.rearrange("p (a b) -> p a b", b=shape[2])
        elif len(shape) == 4:
            v = v.rearrange("p (a b c) -> p a b c", b=shape[2], c=shape[3])
        return v


def build():
    nc = bass.Bass("TRN2", target_bir_lowering=False)

    def din(name, shape, dt=F32):
        return nc.dram_tensor(name, shape, dt, kind="ExternalInput").ap()
    x_d = din("x", [S, D])
    p_d = din("p", [S, 256])
    pos_d = din("pos", [S], I32)
    freq_d = din("freq", [32, 1])
    g_pre_d = din("norm_pre_g", [D])
    w_in_d = din("w_in", [D, 2976])
    g_q_d = din("q_norm_g", [256])
    w_uq_d = din("w_uq", [256, 768])
    g_kv_d = din("kv_norm_g", [128])
    w_ukv_d = din("w_ukv", [128, 1024])
    g_sbo_d = din("sb_out_norm_g", [512])
    g_mlo_d = din("mla_out_norm_g", [512])
    w_out_d = din("w_out", [D, D])
    g_post_d = din("norm_post_g", [D])
    w_ple_d = din("w_ple", [256, D])
    g_ple_d = din("ple_norm_g", [D])
    w_pg_d = din("w_ple_gate", [D, D])
    b_pg_d = din("b_ple_gate", [D])
    out_d = nc.dram_tensor("out", [S, D], F32, kind="ExternalOutput").ap()
    if DBG:
        dbg_sb_d = nc.dram_tensor("dbg_sb", [S, 512], F32, kind="ExternalOutput").ap()
        dbg_ml_d = nc.dram_tensor("dbg_ml", [S, 512], F32, kind="ExternalOutput").ap()
        dbg_x1_d = nc.dram_tensor("dbg_x1", [S, D], F32, kind="ExternalOutput").ap()
        dbg_pl_d = nc.dram_tensor("dbg_pl", [S, D], F32, kind="ExternalOutput").ap()
        dbg_sg_d = nc.dram_tensor("dbg_sg", [S, D], F32, kind="ExternalOutput").ap()

    with ExitStack() as st:
        P = Prog(nc, st)
        ARENA_B = 204 * 1024
        arena_t = st.enter_context(nc.sbuf_tensor("arena", [128, ARENA_B // 2], BF16))
        arena = arena_t[:]
        ps_all = st.enter_context(nc.psum_tensor("ps_all", [128, 4096], F32))
        psb = [ps_all[:, i * 512:(i + 1) * 512] for i in range(8)]
        K = 1024
        R_const = Region(arena, 0, 4 * K)
        R_osb = Region(arena, 4 * K, 36 * K)
        R_oml = Region(arena, 36 * K, 68 * K)
        R_big = Region(arena, 68 * K, 164 * K)
        R_tmp = Region(arena, 164 * K, 204 * K)
        R_o2 = Region(arena, 4 * K, 68 * K)

        def MM(out, lhsT, rhs, start, stop, r, w):
            P.op("pe", lambda e: e.matmul(out, lhsT=lhsT, rhs=rhs, start=start, stop=stop), r, w)

        def TR(out, in_, r, w):
            P.op("pe", lambda e: e.transpose(out=out, in_=in_, identity=ident), list(r) + ["ident"], w)

        def ACT(out, in_, func, r, w, scale=1.0, bias=0.0, accum=None):
            P.op("act", lambda e: e.activation(out=out, in_=in_, func=func, bias=bias, scale=scale,
                                               accum_out=accum), r, w, fence=(accum is not None))

        def TS(eng, out, in0, s1, s2, op0, op1, r, w, fence=False):
            if s2 is None:
                P.op(eng, lambda e: e.tensor_scalar(out=out, in0=in0, scalar1=s1, scalar2=None, op0=op0), r, w, fence=fence)
            else:
                P.op(eng, lambda e: e.tensor_scalar(out=out, in0=in0, scalar1=s1, scalar2=s2, op0=op0, op1=op1), r, w, fence=fence)

        def TT(eng, out, in0, in1, op, r, w, fence=False):
            P.op(eng, lambda e: e.tensor_tensor(out=out, in0=in0, in1=in1, op=op), r, w, fence=fence)

        def STT(eng, out, in0, scalar, in1, op0, op1, r, w):
            P.op(eng, lambda e: e.scalar_tensor_tensor(out=out, in0=in0, scalar=scalar, in1=in1, op0=op0, op1=op1), r, w)

        def CP(eng, out, in_, r, w):
            if eng == "act":
                P.op("act", lambda e: e.copy(out=out, in_=in_), r, w)
            else:
                P.op(eng, lambda e: e.tensor_copy(out=out, in_=in_), r, w)

        def MSET(eng, ap, val, w):
            P.op(eng, lambda e: e.memset(ap, val), [], w)

        fsc = {e: R_const.alloc([128, 8], F32) for e in ("act", "dve", "pool")}
        rsq_tmp = [R_const.alloc([128, 16], F32) for _ in range(4)]
        rsq_i = [0]

        def RSQ(out, in_, c, n, r, w):
            k = rsq_i[0] % 4
            rsq_i[0] += 1
            t = rsq_tmp[k][0:in_.shape[0], 0:n]
            own = P.cur_gid is None
            if own:
                P.group_begin()
            TS("pool", t, in_, float(c), None, ALU.add, None, r, ["rsqt%d" % k], fence=True)
            TT("pool", out, t, neghalf[0:in_.shape[0], 0:n], ALU.pow, ["rsqt%d" % k, "neghalf"], w, fence=True)
            if own:
                P.group_end()

        def RSQ2(out, a, b, c, r, w):
            k = rsq_i[0] % 4
            rsq_i[0] += 1
            t = rsq_tmp[k][:, 0:1]
            t2 = rsq_tmp[k][:, 8:9]
            own = P.cur_gid is None
            if own:
                P.group_begin()
            TT("pool", t, a, b, ALU.add, r, ["rsqt%d" % k], fence=True)
            TS("pool", t2, t, float(c), None, ALU.add, None, ["rsqt%d" % k], ["rsqu%d" % k], fence=True)
            TT("pool", out, t2, neghalf[:, 0:1], ALU.pow, ["rsqu%d" % k, "neghalf"], w, fence=True)
            if own:
                P.group_end()

        def DMA(eng, sem, out, in_, r, w, slow=False):
            if slow:
                P.dma(eng, sem, lambda e: e.dma_start(out=out, in_=in_, allow_slow_non_contiguous=True), r, w)
            else:
                P.dma(eng, sem, lambda e: e.dma_start(out=out, in_=in_), r, w)

        def pbf(i):
            return psb[i][:].bitcast(BF16)

        ident = R_const.alloc([128, 128], BF16)
        maskZ = R_const.alloc([128, 128], BF16)
        maskM = R_const.alloc([128, 128], F32)
        ones_f = R_const.alloc([128, 512], F32)
        g_pre_t = R_const.alloc([128, 8], F32)
        g_q_t = R_const.alloc([128, 2], F32)
        g_kv_t = R_const.alloc([128, 1], F32)
        g_o_t = R_const.alloc([128, 8], F32)
        freq_t = R_const.alloc([32, 1], F32, parts=32)
        ones_row = R_const.alloc([1, 128], BF16)
        neghalf = R_const.alloc([128, 16], F32)
        epsb = {}
        for cval in (D * EPS, 256 * EPS, 128 * EPS, 64 * EPS):
            epsb[cval] = R_const.alloc([128, 1], F32)
        sem_c = P.new_dma_sem("d_const")
        sem_w = [P.new_dma_sem("d_w0"), P.new_dma_sem("d_w1")]
        sem_x = [P.new_dma_sem("d_x0"), P.new_dma_sem("d_x1")]
        sem_p = [P.new_dma_sem("d_p0"), P.new_dma_sem("d_p1"), P.new_dma_sem("d_p2")]
        sem_o = [P.new_dma_sem("d_o0"), P.new_dma_sem("d_o1")]
        sem_pos = P.new_dma_sem("d_pos")
        sem_dbg = P.new_dma_sem("d_dbg")

        tmpf = R_tmp.alloc([128, 128], F32)
        MSET("pool", tmpf, 0.0, ["tmpf"])
        P.op("pool", lambda e: e.affine_select(out=tmpf, in_=tmpf, pattern=[[-1, 128]], compare_op=ALU.not_equal,
                                                fill=1.0, base=0, channel_multiplier=1), ["tmpf"], ["tmpf"])
        CP("dve", ident, tmpf, ["tmpf"], ["ident"])
        MSET("pool", maskM, 1.0, ["maskM"])
        P.op("pool", lambda e: e.affine_select(out=maskM, in_=maskM, pattern=[[-1, 128]], compare_op=ALU.is_ge,
                                                fill=0.0, base=0, channel_multiplier=1), ["maskM"], ["maskM"])
        TS("dve", maskZ, maskM, -1.0, 1.0, ALU.mult, ALU.add, ["maskM"], ["maskZ"])
        MSET("pool", ones_f, 1.0, ["ones_f"])
        MSET("pool", ones_row, 1.0, ["ones_row"])
        MSET("pool", neghalf, -0.5, ["neghalf"])
        MSET("pool", fsc["act"], 0.0, ["fsc"])
        for cval, tl in epsb.items():
            MSET("pool", tl, float(cval), ["epsb"])
        DMA("sp", sem_c, g_pre_t, g_pre_d.rearrange("(k p) -> p k", p=128), [], ["g_pre_t", "cser"], slow=True)
        DMA("sp", sem_c, g_q_t, g_q_d.rearrange("(k p) -> p k", p=128), [], ["g_q_t", "cser"], slow=True)
        DMA("sp", sem_c, g_kv_t, g_kv_d.rearrange("(k p) -> p k", p=128), [], ["g_kv_t", "cser"], slow=True)
        DMA("sp", sem_c, g_o_t[:, 0:4], g_sbo_d.rearrange("(k p) -> p k", p=128), [], ["g_o_t", "cser"], slow=True)
        DMA("sp", sem_c, g_o_t[:, 4:8], g_mlo_d.rearrange("(k p) -> p k", p=128), [], ["g_o_t", "cser"], slow=True)
        DMA("sp", sem_c, freq_t, freq_d, [], ["freq_t", "cser"])

        wslot = [0]

        def load_w(dst, src, ncols, scale, stage, tag):
            i = wslot[0] % 2
            wslot[0] += 1
            stg = stage[i][:, 0:ncols]
            DMA("sp", sem_w[i], stg, src, [], ["wst%d" % i])
            rd = ["wst%d" % i, "g_pre_t", "g_q_t", "g_kv_t", "g_o_t"]
            if i == 0:
                if scale is None:
                    CP("dve", dst, stg, rd, [tag])
                else:
                    TS("dve", dst, stg, scale, None, ALU.mult, None, rd, [tag])
            else:
                if scale is None:
                    CP("act", dst, stg, rd, [tag])
                else:
                    ACT(dst, stg, AF.Copy, rd, [tag], scale=scale)

        def norm_transpose_block(blk, xb, xs, junk, ssq, rq, hT, hT_tag, col0, tp_bank, xi, evac_eng):
            DMA("sp", sem_x[xi], xb, x_d[blk * 128:(blk + 1) * 128, :], [], ["xb%d" % xi])
            ACT(junk, xb, AF.Square, ["xb%d" % xi], ["ssq"], accum=ssq)
            RSQ(rq, ssq, D * EPS, 1, ["ssq"], ["rq"])
            TS("dve", xs, xb, rq[:, 0:1], 32.0, ALU.mult, ALU.mult, ["xb%d" % xi, "rq"], ["xs"])
            for kc in range(8):
                TR(pbf(tp_bank)[:, kc * 128:(kc + 1) * 128], xs[:, kc * 128:(kc + 1) * 128], ["xs"], ["ps%d" % tp_bank])
            CP(evac_eng, hT[:, :, col0:col0 + 128], pbf(tp_bank).rearrange("p (k t) -> p k t", t=128),
               ["ps%d" % tp_bank], [hT_tag])

        QT = R_big.alloc([128, 4, S], BF16)
        KT = R_big.alloc([128, 4, S], BF16)
        dV = R_big.alloc([128, NB, 512], BF16)
        R_o2.reset()
        Wqkv = R_o2.alloc([128, 8, 1536], BF16)
        hTc = [R_o2.alloc([128, 8, 513], BF16), R_o2.alloc([128, 8, 513], BF16)]
        dhT = R_o2.alloc([128, 8, 512], BF16)
        xbuf = [R_o2.alloc([128, 1024], F32), R_o2.alloc([128, 1024], F32)]
        xs1 = R_tmp.alloc([128, 1024], BF16)
        junk = R_tmp.alloc([128, 1024], BF16)
        ssq = R_tmp.alloc([128, 1], F32)
        rq = R_tmp.alloc([128, 1], F32)
        wst = [R_tmp.alloc([128, 1536], F32), R_tmp.alloc([128, 1536], F32)]

        def p1_weights():
            for kc in range(8):
                load_w(Wqkv[:, kc, :], w_in_d[kc * 128:(kc + 1) * 128, 0:1536], 1536, g_pre_t[:, kc:kc + 1], wst, "Wqkv")

        xcnt1 = [0]

        def p1_norm(ci):
            c = NCH - 1 - ci
            hT = hTc[ci % 2]
            htag = "hTc%d" % (ci % 2)
            for j in range(4):
                xi = xcnt1[0] % 2
                xcnt1[0] += 1
                norm_transpose_block(c * 4 + j, xbuf[xi], xs1, junk, ssq, rq, hT, htag, j * 128, 6 + (j % 2), xi,
                                     "act" if j % 2 else "dve")
            if ci == 0:
                MSET("pool", hT[:, :, 512:513], 0.0, [htag])
            else:
                CP("pool", hT[:, :, 512:513], hTc[(ci - 1) % 2][:, :, 0:1], ["hTc%d" % ((ci - 1) % 2)], [htag])

        def p1_mm(ci):
            c = NCH - 1 - ci
            hT = hTc[ci % 2]
            htag = "hTc%d" % (ci % 2)
            TT("pool", dhT, hT[:, :, 1:513], hT[:, :, 0:512], ALU.subtract, [htag], ["dhT"])
            n_ev = 0
            for which, dst, dtag in ((0, QT, "QT"), (1, KT, "KT")):
                for f in range(4):
                    bank = n_ev % 4
                    for kc in range(8):
                        MM(psb[bank][:, :], Wqkv[:, kc, which * 512 + f * 128: which * 512 + (f + 1) * 128],
                           hT[:, kc, 0:512], kc == 0, kc == 7, ["Wqkv", htag], ["ps%d" % bank])
                    CP("act" if n_ev % 2 else "dve", dst[:, f, c * 512:(c + 1) * 512], psb[bank][:, :],
                       ["ps%d" % bank], [dtag])
                    n_ev += 1
            for j in range(4):
                bank = n_ev % 4
                for kc in range(8):
                    MM(psb[bank][:, :], dhT[:, kc, j * 128:(j + 1) * 128], Wqkv[:, kc, 1024:1536],
                       kc == 0, kc == 7, ["Wqkv", "dhT"], ["ps%d" % bank])
                CP("act" if n_ev % 2 else "dve", dV[:, c * 4 + j, :], psb[bank][:, :], ["ps%d" % bank], ["dV"])
                n_ev += 1

        P.replay_interleaved([P.collect(p1_weights), P.collect(p1_norm, 0)])
        for ci in range(NCH):
            lists = [P.collect(p1_mm, ci)]
            if ci + 1 < NCH:
                lists.append(P.collect(p1_norm, ci + 1))
            P.replay_interleaved(lists)
        P.barrier()

        Osb = R_osb.alloc([128, NB, 512], BF16)
        R_tmp.reset()
        NU = 4
        u_sb = [R_tmp.alloc([128, 512], F32) for _ in range(NU)]
        Pb = [[R_tmp.alloc([128, 512], BF16) for _ in range(2)] for _ in range(8)]
        NPT = 5
        PTs = [R_tmp.alloc([128, 512], BF16) for _ in range(NPT)]
        QAB = [[R_tmp.alloc([128, 2, 128], BF16) for _ in range(4)] for _ in range(2)]
        z512 = R_tmp.alloc([1, 512], BF16)
        MSET("pool", z512, 0.0, ["z512"])
        negmask = R_tmp.alloc([128, 128], BF16)
        TS("dve", negmask, maskM, -30000.0, None, ALU.mult, None, ["maskM"], ["negmask"])
        for par in range(2):
            for g_ in range(4):
                MSET("pool", QAB[par][g_][64:128, 0, :], 0.0, ["QAB%d_%d" % (par, g_)])
                MSET("pool", QAB[par][g_][0:64, 1, :], 0.0, ["QAB%d_%d" % (par, g_)])
        items = []
        for qb in range(NB):
            nkb = NB - qb
            ntile = (nkb + 3) // 4
            for tau in range(ntile):
                kb0 = qb + 4 * tau
                nblk = min(4, NB - kb0)
                for h in range(8):
                    items.append((qb, tau, kb0, nblk, h, tau == ntile - 1))
        NI = len(items)
        LA = 4
        LF = 3

        def sb_stageA(n):
            qb, tau, kb0, nblk, h, last = items[n]
            g, half = h // 2, h % 2
            zb = n % 3
            w = nblk * 128
            rows = slice(64 * half, 64 * half + 64)
            qtile = QAB[qb % 2][g]
            qabtag = "QAB%d_%d" % (qb % 2, g)
            if tau == 0 and h == 0:
                for qn in ([0, 1] if qb == 0 else [qb + 1]):
                    if qn < NB:
                        for g2 in range(4):
                            qt2 = QAB[qn % 2][g2]
                            tg2 = "QAB%d_%d" % (qn % 2, g2)
                            CP("pool", qt2[0:64, 0, :], QT[0:64, g2, qn * 128:(qn + 1) * 128], ["QT"], [tg2])
                            CP("pool", qt2[64:128, 1, :], QT[64:128, g2, qn * 128:(qn + 1) * 128], ["QT"], [tg2])
            MM(psb[zb][:, 0:w], qtile[:, half, :], KT[:, g, kb0 * 128: kb0 * 128 + w],
               True, tau != 0, [qabtag, "KT"], ["ps%d" % zb])
            if tau == 0:
                MM(psb[zb][:, 0:128], ident, negmask, False, True, ["ident", "negmask"], ["ps%d" % zb])
            ub = n % NU
            ACT(u_sb[ub][:, 0:w], psb[zb][:, 0:w], AF.Sigmoid, ["ps%d" % zb], ["u%d" % ub], scale=-0.125)
            pbuf = Pb[h][tau % 2]
            ptag = "Pb%d_%d" % (h, tau % 2)
            if tau == 0:
                P.op("dve", lambda e: e.tensor_tensor_scan(out=pbuf[:, 0:w], data0=u_sb[ub][:, 0:w], data1=ones_f[:, 0:w],
                                                           initial=1.0, op0=ALU.mult, op1=ALU.mult),
                     ["u%d" % ub, "ones_f"], [ptag])
                TT("pool", pbuf[:, 0:128], pbuf[:, 0:128], maskZ, ALU.mult, [ptag, "maskZ"], [ptag])
            else:
                prev = Pb[h][(tau - 1) % 2]
                P.op("dve", lambda e: e.tensor_tensor_scan(out=pbuf[:, 0:w], data0=u_sb[ub][:, 0:w], data1=ones_f[:, 0:w],
                                                           initial=prev[:, 511:512], op0=ALU.mult, op1=ALU.mult),
                     ["u%d" % ub, "ones_f", "Pb%d_%d" % (h, (tau - 1) % 2)], [ptag])

        def sb_stageD(n):
            qb, tau, kb0, nblk, h, last = items[n]
            w = nblk * 128
            pbuf = Pb[h][tau % 2]
            ptag = "Pb%d_%d" % (h, tau % 2)
            tb = 3 + (n % 2)
            for k in range(nblk):
                TR(pbf(tb)[:, k * 128:(k + 1) * 128], pbuf[:, k * 128:(k + 1) * 128], [ptag], ["ps%d" % tb])
            pt = n % NPT
            CP("act", PTs[pt][:, 0:w], pbf(tb)[:, 0:w], ["ps%d" % tb], ["PT%d" % pt])

        def sb_stageF(n):
            qb, tau, kb0, nblk, h, last = items[n]
            pt = n % NPT
            ob = 5 + (qb % 2)
            if tau == 0 and h == 0:
                MM(psb[ob][:, :], z512[0:1, 0:128], z512[0:1, :], True, False, ["z512"], ["ps%d" % ob])
            for k in range(nblk):
                MM(psb[ob][:, h * 64:(h + 1) * 64], PTs[pt][:, k * 128:(k + 1) * 128],
                   dV[:, kb0 + k, h * 64:(h + 1) * 64], False, last and h == 7 and k == nblk - 1,
                   ["PT%d" % pt, "dV"], ["ps%d" % ob])
            if last and h == 7:
                CP("dve", Osb[:, qb, :], psb[ob][:, :], ["ps%d" % ob], ["Osb"])

        for step in range(NI + LA + LF):
            if step < NI:
                sb_stageA(step)
            if 0 <= step - LA < NI:
                sb_stageD(step - LA)
            if 0 <= step - LA - LF < NI:
                sb_stageF(step - LA - LF)
        P.barrier()

        R_big.reset()
        KnT = R_big.alloc([128, 4, S], BF16)
        Vaug = R_big.alloc([128, NB, 8, 65], BF16)
        cqnT = R_big.alloc([128, 2, S], BF16)
        KrT = R_big.alloc([128, S], BF16)
        R_oml.reset()
        R_tmp.reset()
        Wmla = R_oml.alloc([128, 8, 448], BF16)
        hT3r = [R_oml.alloc([128, 8, 512], BF16) for _ in range(2)]
        xbuf = [R_oml.alloc([128, 1024], F32), R_oml.alloc([128, 1024], F32)]
        ckvT = R_oml.alloc([128, 512], BF16)
        WuqN = R_tmp.alloc([128, 2, 512], BF16)
        WuqRA = R_tmp.alloc([128, 2, 256], BF16)
        WuqRB = R_tmp.alloc([128, 2, 256], BF16)
        posi = R_tmp.alloc([32, 512], I32, parts=32)
        ang = R_tmp.alloc([32, 512], F32, parts=32)
        ang2 = R_tmp.alloc([32, 512], F32, parts=32)
        cosT = R_tmp.alloc([32, 512], F32, parts=32)
        sinT = R_tmp.alloc([32, 512], F32, parts=32)
        rt1 = R_tmp.alloc([32, 512], F32, parts=32)
        rt2 = R_tmp.alloc([32, 512], F32, parts=32)
        negpi = R_tmp.alloc([32, 1], F32, parts=32)
        tmp_mark = R_tmp.ptr
        WukvK = R_tmp.alloc([128, 512], BF16)
        WukvV = R_tmp.alloc([128, 512], BF16)
        xs1 = R_tmp.alloc([128, 1024], BF16)
        junk = R_tmp.alloc([128, 1024], BF16)
        ssq = R_tmp.alloc([128, 1], F32)
        rq = R_tmp.alloc([128, 1], F32)
        ssq2a = R_tmp.alloc([128, 16], F32)
        ssq2b = R_tmp.alloc([128, 16], F32)
        rq2a = R_tmp.alloc([128, 16], F32)
        rq2b = R_tmp.alloc([128, 16], F32)
        cqs = R_tmp.alloc([128, 384], BF16)
        wst = [R_tmp.alloc([128, 1024], F32), R_tmp.alloc([128, 1024], F32)]

        def rope_tables(c, tag, tabs=None):
            sin_d, cos_d = tabs if tabs is not None else (sinT, cosT)
            DMA("sp", sem_pos, posi, pos_d[c * 512:(c + 1) * 512].partition_broadcast(32), [], ["posi"])
            CP("dve", ang, posi, ["posi"], ["ang"])
            TS("dve", ang, ang, freq_t[:, 0:1], None, ALU.mult, None, ["ang", "freq_t"], ["ang"])
            C1 = 6.28125
            C2 = float(2 * np.pi - 6.28125)
            for shift, dst, dtag in ((0.0, sin_d, "sinT" + tag), (float(np.pi / 2), cos_d, "cosT" + tag)):
                if shift:
                    TS("dve", ang2, ang, shift, None, ALU.add, None, ["ang"], ["ang2"])
                    src = ang2
                else:
                    src = ang
                TS("dve", rt1, src, float(1.0 / (2 * np.pi)), None, ALU.mult, None, ["ang", "ang2"], ["rt1"])
                CP("dve", posi, rt1, ["rt1"], ["posi"])
                CP("dve", rt1, posi, ["posi"], ["rt1"])
                STT("dve", rt2, rt1, -C1, src, ALU.mult, ALU.add, ["rt1", "ang", "ang2"], ["rt2"])
                STT("dve", rt2, rt1, -C2, rt2, ALU.mult, ALU.add, ["rt1", "rt2"], ["rt2"])
                TS("dve", rt2, rt2, float(np.pi), float(-np.pi), ALU.min, ALU.max, ["rt2"], ["rt2"])
                ACT(dst, rt2, AF.Sin, ["rt2"], [dtag])


        def p3_weights():
            for kc in range(8):
                i = wslot[0] % 2
                wslot[0] += 1
                stg = wst[i]
                DMA("sp", sem_w[i], stg[:, 0:416], w_in_d[kc * 128:(kc + 1) * 128, 2048:2464], [], ["wst%d" % i])
                sc = g_pre_t[:, kc:kc + 1]
                TS("dve", Wmla[:, kc, 0:416], stg[:, 0:416], sc, None, ALU.mult, None, ["wst%d" % i, "g_pre_t"], ["Wmla"])
                TS("dve", Wmla[:, kc, 416:432], stg[:, 400:416], sc, -1.0, ALU.mult, ALU.mult, ["wst%d" % i, "g_pre_t"], ["Wmla"])
                TS("dve", Wmla[:, kc, 432:448], stg[:, 384:400], sc, None, ALU.mult, None, ["wst%d" % i, "g_pre_t"], ["Wmla"])
            for kc in range(2):
                i = wslot[0] % 2
                wslot[0] += 1
                stg = wst[i]
                DMA("sp", sem_w[i], stg[:, 0:768], w_uq_d[kc * 128:(kc + 1) * 128, :], [], ["wst%d" % i])
                sc = g_q_t[:, kc:kc + 1]
                sv = stg[:, 0:768].rearrange("p (h c) -> p h c", c=96)
                rd, wr = ["wst%d" % i, "g_q_t"], ["Wuq"]
                TS("dve", WuqN[:, kc, :].rearrange("p (h c) -> p h c", c=64), sv[:, :, 0:64], sc, None, ALU.mult, None, rd, wr)
                TS("dve", WuqRA[:, kc, :].rearrange("p (h c) -> p h c", c=32), sv[:, :, 64:96], sc, None, ALU.mult, None, rd, wr)
                rb = WuqRB[:, kc, :].rearrange("p (h c) -> p h c", c=32)
                TS("dve", rb[:, :, 0:16], sv[:, :, 80:96], sc, -1.0, ALU.mult, ALU.mult, rd, wr)
                TS("dve", rb[:, :, 16:32], sv[:, :, 64:80], sc, None, ALU.mult, None, rd, wr)
            i = wslot[0] % 2
            wslot[0] += 1
            DMA("sp", sem_w[i], wst[i][:, 0:1024], w_ukv_d[:, :], [], ["wst%d" % i])
            sv = wst[i][:, 0:1024].rearrange("p (h c) -> p h c", c=128)
            TS("dve", WukvK.rearrange("p (h c) -> p h c", c=64), sv[:, :, 0:64], g_kv_t[:, 0:1], None, ALU.mult, None,
               ["wst%d" % i, "g_kv_t"], ["Wukv"])
            TS("dve", WukvV.rearrange("p (h c) -> p h c", c=64), sv[:, :, 64:128], g_kv_t[:, 0:1], None, ALU.mult, None,
               ["wst%d" % i, "g_kv_t"], ["Wukv"])
        MSET("pool", Vaug[:, :, :, 64:65], 1.0, ["Vaug"])
        MSET("pool", KrT[32:64, :], 0.0, ["KrT"])
        MSET("pool", KrT[64:128, :], 0.0, ["KrT"])

        xcnt3 = [0]

        def p3_norm(c):
            hT3 = hT3r[c % 2]
            for j in range(4):
                xi = xcnt3[0] % 2
                xcnt3[0] += 1
                norm_transpose_block(c * 4 + j, xbuf[xi], xs1, junk, ssq, rq, hT3, "hT3_%d" % (c % 2), j * 128, 6 + (j % 2), xi,
                                     "act" if j % 2 else "dve")

        def p3_mm(c):
            hT3 = hT3r[c % 2]
            h3tag = "hT3_%d" % (c % 2)
            rope_tables(c, "")
            for kc in range(8):
                MM(psb[4][0:32, :], Wmla[:, kc, 384:416], hT3[:, kc, :], kc == 0, kc == 7, ["Wmla", h3tag], ["ps4"])
            for kc in range(8):
                MM(psb[5][0:32, :], Wmla[:, kc, 416:448], hT3[:, kc, :], kc == 0, kc == 7, ["Wmla", h3tag], ["ps5"])
            TT("dve", rt1, psb[4][0:32, :], cosT, ALU.mult, ["ps4", "cosT"], ["rt1"])
            TT("dve", rt2, psb[5][0:32, :], sinT, ALU.mult, ["ps5", "sinT"], ["rt2"])
            TT("dve", KrT[0:32, c * 512:(c + 1) * 512], rt1, rt2, ALU.add, ["rt1", "rt2"], ["KrT"])
            for j in range(4):
                bank = j % 2
                for kc in range(8):
                    MM(psb[bank][:, 0:384], hT3[:, kc, j * 128:(j + 1) * 128], Wmla[:, kc, 0:384], kc == 0, kc == 7,
                       ["Wmla", h3tag], ["ps%d" % bank])
                ACT(junk[:, 0:256], psb[bank][:, 0:256], AF.Square, ["ps%d" % bank], ["ssq2a"], accum=ssq2a[:, 0:1])
                ACT(junk[:, 256:384], psb[bank][:, 256:384], AF.Square, ["ps%d" % bank], ["ssq2b"], accum=ssq2b[:, 0:1])
                RSQ(rq2a[:, 0:1], ssq2a[:, 0:1], 256 * EPS, 1, ["ssq2a"], ["rq2a"])
                RSQ(rq2b[:, 0:1], ssq2b[:, 0:1], 128 * EPS, 1, ["ssq2b"], ["rq2b"])
                TS("dve", cqs[:, 0:256], psb[bank][:, 0:256], rq2a[:, 0:1], 16.0, ALU.mult, ALU.mult,
                   ["ps%d" % bank, "rq2a"], ["cqs"])
                TS("dve", cqs[:, 256:384], psb[bank][:, 256:384], rq2b[:, 0:1], float(np.sqrt(128.0)), ALU.mult, ALU.mult,
                   ["ps%d" % bank, "rq2b"], ["cqs"])
                tb = 2 + (j % 2)
                for k3 in range(3):
                    TR(pbf(tb)[:, k3 * 128:(k3 + 1) * 128], cqs[:, k3 * 128:(k3 + 1) * 128], ["cqs"], ["ps%d" % tb])
                tok = c * 512 + j * 128
                CP("act", cqnT[:, :, tok:tok + 128], pbf(tb)[:, 0:256].rearrange("p (k t) -> p k t", t=128),
                   ["ps%d" % tb], ["cqnT"])
                CP("act", ckvT[:, j * 128:(j + 1) * 128], pbf(tb)[:, 256:384], ["ps%d" % tb], ["ckvT"])
            for g in range(4):
                bank = g % 2
                MM(psb[bank][:, :], WukvK[:, g * 128:(g + 1) * 128], ckvT[:, :], True, True, ["Wukv", "ckvT"], ["ps%d" % bank])
                CP("act" if g % 2 else "dve", KnT[:, g, c * 512:(c + 1) * 512], psb[bank][:, :], ["ps%d" % bank], ["KnT"])
            for j in range(4):
                bank = 4 + (j % 2)
                MM(psb[bank][:, :], ckvT[:, j * 128:(j + 1) * 128], WukvV[:, :], True, True, ["Wukv", "ckvT"], ["ps%d" % bank])
                CP("act" if j % 2 else "dve", Vaug[:, c * 4 + j, :, 0:64],
                   psb[bank][:, :].rearrange("p (h c) -> p h c", c=64), ["ps%d" % bank], ["Vaug"])

        P.replay_interleaved([P.collect(p3_weights), P.collect(p3_norm, 0)])
        for c in range(NCH):
            lists = [P.collect(p3_mm, c)]
            if c + 1 < NCH:
                lists.append(P.collect(p3_norm, c + 1))
            if SEQ3:
                for l in lists:
                    P.replay_interleaved([l])
            else:
                P.replay_interleaved(lists)
        P.barrier()

        R_oml.reset()
        Oml = R_oml.alloc([128, NB, 512], BF16)
        R_tmp.ptr = tmp_mark
        QnT = [R_tmp.alloc([128, 8, 512], BF16)] * 2
        QrT = [R_tmp.alloc([128, 8, 512], BF16)] * 2
        for hh in range(8):
            if hh % 2 == 0:
                MSET("pool", QnT[0][64:128, hh, :], 0.0, ["QnT%d" % hh])
            else:
                MSET("pool", QnT[0][0:64, hh, :], 0.0, ["QnT%d" % hh])
        MSET("pool", QrT[0][32:64, :, :], 0.0, ["QrT%d" % hh for hh in range(8)])
        MSET("pool", QrT[0][64:128, :, :], 0.0, ["QrT%d" % hh for hh in range(8)])
        NAT = 5
        ATs = [R_big.alloc([128, 512], BF16) for _ in range(NAT)]
        rec = R_tmp.alloc([128, 4], F32)
        z512b = R_tmp.alloc([1, 512], BF16)
        MSET("pool", z512b, 0.0, ["z512b"])
        SCALE = float(96 ** -0.5)
        an = [0]
        sinT2 = R_tmp.alloc([32, 512], F32, parts=32)
        cosT2 = R_tmp.alloc([32, 512], F32, parts=32)
        cs = [(sinT, cosT), (sinT2, cosT2)]

        def q_tables(qc):
            rope_tables(qc, "q%d" % (qc % 2), tabs=cs[qc % 2])

        def qn_proj(qc, g):
            for kc in range(2):
                MM(psb[5][:, :], WuqN[:, kc, g * 128:(g + 1) * 128], cqnT[:, kc, qc * 512:(qc + 1) * 512],
                   kc == 0, kc == 1, ["Wuq", "cqnT"], ["ps5"])
            CP("dve", QnT[0][0:64, 2 * g, :], psb[5][0:64, :], ["ps5"], ["QnT%d" % (2 * g)])
            CP("dve", QnT[0][64:128, 2 * g + 1, :], psb[5][64:128, :], ["ps5"], ["QnT%d" % (2 * g + 1)])

        def rope_head(qc, h):
            sin_d, cos_d = cs[qc % 2]
            sfx = "q%d" % (qc % 2)
            for kc in range(2):
                MM(psb[6][0:32, :], WuqRA[:, kc, h * 32:(h + 1) * 32], cqnT[:, kc, qc * 512:(qc + 1) * 512],
                   kc == 0, kc == 1, ["Wuq", "cqnT"], ["ps6"])
            for kc in range(2):
                MM(psb[7][0:32, :], WuqRB[:, kc, h * 32:(h + 1) * 32], cqnT[:, kc, qc * 512:(qc + 1) * 512],
                   kc == 0, kc == 1, ["Wuq", "cqnT"], ["ps7"])
            TT("dve", rt1, psb[6][0:32, :], cos_d, ALU.mult, ["ps6", "cosT" + sfx], ["rt1"])
            TT("dve", rt2, psb[7][0:32, :], sin_d, ALU.mult, ["ps7", "sinT" + sfx], ["rt2"])
            TT("dve", QrT[0][0:32, h, :], rt1, rt2, ALU.add, ["rt1", "rt2"], ["QrT%d" % h])

        q_tables(0)
        for g in range(4):
            qn_proj(0, g)
        for h in range(8):
            rope_head(0, h)

        for qc in range(NCH):
            qi = 0
            if qc + 1 < NCH:
                q_tables(qc + 1)

            mitems = []
            for h in range(8):
                for kb in range(4 * qc, NB):
                    mitems.append((h, kb))
            LM = 4

            def ml_stageA(m):
                h, kb = mitems[m]
                g, half = h // 2, h % 2
                rows = slice(64 * half, 64 * half + 64)
                i = kb - 4 * qc
                ncols = 512 if i >= 4 else (i + 1) * 128
                n = an[0] + m
                zb = n % 3
                MM(psb[zb][:, 0:ncols], KnT[:, g, kb * 128:(kb + 1) * 128], QnT[qi][:, h, 0:ncols], True, False,
                   ["KnT", "QnT%d" % h], ["ps%d" % zb])
                MM(psb[zb][:, 0:ncols], KrT[:, kb * 128:(kb + 1) * 128], QrT[qi][:, h, 0:ncols], False, True,
                   ["KrT", "QrT%d" % h], ["ps%d" % zb])
                at = n % NAT
                A = ATs[at]
                atag = "AT%d" % at
                if i >= 4:
                    ACT(A[:, :], psb[zb][:, :], AF.Exp, ["ps%d" % zb], [atag], scale=SCALE)
                else:
                    c0 = i * 128
                    ACT(A[:, 0:c0 + 128], psb[zb][:, 0:c0 + 128], AF.Exp, ["ps%d" % zb], [atag], scale=SCALE)
                    MSET("pool", A[0:64, c0 + 64:c0 + 128], 0.0, [atag])

            def ml_stageD(m):
                h, kb = mitems[m]
                i = kb - 4 * qc
                nj = 4 if i >= 4 else i + 1
                n = an[0] + m
                at = n % NAT
                ob = 3 + (h % 2)
                oacc = psb[ob][:, 0:260].rearrange("p (j c) -> p j c", c=65)
                if kb == 4 * qc:
                    MM(psb[ob][:, 0:260], z512b[0:1, 0:128], z512b[0:1, 0:260], True, False, ["z512b"], ["ps%d" % ob])
                for j in range(nj):
                    MM(oacc[:, j, :], ATs[at][:, j * 128:(j + 1) * 128], Vaug[:, kb, h, :], False, kb == NB - 1 and j == nj - 1,
                       ["AT%d" % at, "Vaug"], ["ps%d" % ob])
                if kb == NB - 1:
                    P.op("dve", lambda e: e.reciprocal(out=rec[:, :].unsqueeze(2), in_=oacc[:, :, 64:65]),
                         ["ps%d" % ob], ["rec"], fence=True)
                    for j in range(4):
                        TS("dve", Oml[:, qc * 4 + j, h * 64:(h + 1) * 64], oacc[:, j, 0:64], rec[:, j:j + 1], None,
                           ALU.mult, None, ["ps%d" % ob, "rec"], ["Oml"])

            NM = len(mitems)
            for step in range(NM + LM):
                if step < NM:
                    ml_stageA(step)
                if 0 <= step - LM < NM:
                    ml_stageD(step - LM)
                    hd, kbd = mitems[step - LM]
                    if kbd == NB - 1 and qc + 1 < NCH:
                        rope_head(qc + 1, hd)
                        if hd % 2 == 1:
                            qn_proj(qc + 1, hd // 2)
            an[0] += NM
        P.barrier()

        R_big.reset()
        R_tmp.reset()
        Wg = R_big.alloc([128, 8, 1024], BF16)
        Wv5 = R_big.alloc([128, 8, 512], BF16)
        Wout = R_big.alloc([128, 8, 1024], BF16)
        Wpg = R_big.alloc([128, 8, 1024], BF16)
        Wple = R_big.alloc([128, 2, 1024], BF16)
        gpost_b = R_big.alloc([128, 1024], F32)
        gple_b = R_big.alloc([128, 1024], F32)
        bpg_b = R_big.alloc([1, 1024], BF16)
        hTb = [R_big.alloc([128, 8, 129], BF16) for _ in range(3)]
        yT = R_big.alloc([128, 8, 128], BF16)
        x1T = R_big.alloc([128, 8, 128], BF16)
        pT = R_big.alloc([128, 2, 128], BF16)
        pbb = R_big.alloc([128, 256], BF16)
        ybr = [R_big.alloc([128, 1024], BF16) for _ in range(2)]
        x1b = R_big.alloc([128, 1024], BF16)
        pbuf = [R_big.alloc([128, 256], F32) for _ in range(4)]
        tmp5 = R_tmp.ptr
        wst = [R_tmp.alloc([128, 1024], F32), R_tmp.alloc([128, 1024], F32)]
        bpg_f = R_tmp.alloc([1, 1024], F32)
        small5 = R_big.alloc([128, 16 * 12], F32)
        ssq = small5[:, 0:1]
        rq = small5[:, 16:17]
        s16 = small5[:, 32:48]
        r16 = small5[:, 48:64]
        sA0 = small5[:, 64:65]
        sA1 = small5[:, 80:81]
        sB = small5[:, 96:97]
        rB = small5[:, 112:113]
        sC0 = small5[:, 128:129]
        sC1 = small5[:, 144:145]
        sD = small5[:, 160:161]
        rD = small5[:, 176:177]

        for kc in range(8):
            i = wslot[0] % 2
            wslot[0] += 1
            DMA("sp", sem_w[i], wst[i][:, 0:512], w_in_d[kc * 128:(kc + 1) * 128, 1536:2048], [], ["wst%d" % i])
            DMA("sp", sem_w[i], wst[i][:, 512:1024], w_in_d[kc * 128:(kc + 1) * 128, 2464:2976], [], ["wst%d" % i])
            if i == 0:
                TS("dve", Wg[:, kc, :], wst[i][:, :], g_pre_t[:, kc:kc + 1], None, ALU.mult, None, ["wst%d" % i, "g_pre_t"], ["Wg"])
            else:
                ACT(Wg[:, kc, :], wst[i][:, :], AF.Copy, ["wst%d" % i, "g_pre_t"], ["Wg"], scale=g_pre_t[:, kc:kc + 1])
        for kc in range(8):
            load_w(Wv5[:, kc, :], w_in_d[kc * 128:(kc + 1) * 128, 1024:1536], 512, g_pre_t[:, kc:kc + 1], wst, "Wv5")
        for kc in range(8):
            load_w(Wout[:, kc, :], w_out_d[kc * 128:(kc + 1) * 128, :], 1024, g_o_t[:, kc:kc + 1], wst, "Wout")
        for kc in range(8):
            load_w(Wpg[:, kc, :], w_pg_d[kc * 128:(kc + 1) * 128, :], 1024, None, wst, "Wpg")
        for kc in range(2):
            load_w(Wple[:, kc, :], w_ple_d[kc * 128:(kc + 1) * 128, :], 1024, None, wst, "Wple")
        DMA("sp", sem_c, gpost_b, g_post_d.partition_broadcast(128), [], ["gpost_b", "cser"])
        DMA("sp", sem_c, gple_b, g_ple_d.partition_broadcast(128), [], ["gple_b", "cser"])
        DMA("sp", sem_c, bpg_f, b_pg_d.partition_broadcast(1), [], ["bpg_f", "cser"])
        TS("dve", gpost_b, gpost_b, 32.0, None, ALU.mult, None, ["gpost_b"], ["gpost_b"])
        TS("dve", gple_b, gple_b, 32.0, None, ALU.mult, None, ["gple_b"], ["gple_b"])
        CP("dve", bpg_b, bpg_f, ["bpg_f"], ["bpg_b"])
        P.barrier()
        R_tmp.ptr = tmp5
        xbuf = [R_tmp.alloc([128, 1024], F32) for _ in range(3)]
        x1r = [R_tmp.alloc([128, 1024], F32) for _ in range(2)]
        plr = [R_tmp.alloc([128, 1024], F32)] * 2
        sg = R_tmp.alloc([128, 1024], F32)
        ofp = R_tmp.alloc([128, 1024], F32)
        sqb = R_tmp.alloc([128, 1024], F32)
        sg2 = R_tmp.alloc([128, 1024], F32)
        xs1 = R_big.alloc([128, 1024], BF16)
        sem_x3 = [sem_x[0], sem_x[1], sem_pos]
        sem_p4 = [sem_p[0], sem_p[1], sem_p[2], sem_w[0]]

        def st_A(bi):
            tb = NB - 1 - bi
            xi = bi % 3
            xb = xbuf[xi]
            hT = hTb[bi % 3]
            htag = "hTb%d" % (bi % 3)
            DMA("sp", sem_p4[bi % 4], pbuf[bi % 4], p_d[tb * 128:(tb + 1) * 128, :], [], ["pb%d" % (bi % 4)])
            DMA("sp", sem_x3[xi], xb, x_d[tb * 128:(tb + 1) * 128, :], [], ["xb%d" % xi])
            ACT(xs1, xb, AF.Square, ["xb%d" % xi], ["ssq", "xs"], accum=ssq)
            RSQ(rq, ssq, D * EPS, 1, ["ssq"], ["rq"])
            TS("dve", xs1, xb, rq[:, 0:1], 32.0, ALU.mult, ALU.mult, ["xb%d" % xi, "rq"], ["xs"])
            P.group_begin()
            for kc in range(8):
                TR(pbf(0)[:, kc * 128:(kc + 1) * 128], xs1[:, kc * 128:(kc + 1) * 128], ["xs"], ["ps0"])
            CP("dve", hT[:, :, 0:128], pbf(0).rearrange("p (k t) -> p k t", t=128), ["ps0"], [htag])
            P.group_end()
            if bi == 0:
                MSET("pool", hT[:, :, 128:129], 0.0, [htag])
            else:
                CP("pool", hT[:, :, 128:129], hTb[(bi - 1) % 3][:, :, 0:1], ["hTb%d" % ((bi - 1) % 3)], [htag])

        def st_B1(bi):
            tb = NB - 1 - bi
            hT = hTb[bi % 3]
            htag = "hTb%d" % (bi % 3)
            yb = ybr[bi % 2]
            ytag = "yb%d" % (bi % 2)
            for half in range(2):
                for kc in range(8):
                    MM(psb[1 + half][:, :], hT[:, kc, 0:128], Wg[:, kc, half * 512:(half + 1) * 512], kc == 0, kc == 7,
                       [htag, "Wg"], ["ps%d" % (1 + half)])
            for kc in range(8):
                MM(psb[3][:, :], hT[:, kc, 1:129], Wv5[:, kc, :], kc == 0, kc == 7, [htag, "Wv5"], ["ps3"])
            ACT(sg[:, 0:512], psb[1][:, :], AF.Silu, ["ps1"], ["sg"])
            ACT(sg[:, 512:1024], psb[2][:, :], AF.Silu, ["ps2"], ["sg"])
            TT("dve", ofp[:, 0:512], psb[3][:, :], Osb[:, tb, :], ALU.add, ["ps3", "Osb"], ["ofp"])
            CP("act", ofp[:, 512:1024], Oml[:, tb, :], ["Oml"], ["ofp"])
            if DBG:
                DMA("sp", sem_dbg, dbg_sb_d[tb * 128:(tb + 1) * 128, :], ofp[:, 0:512], ["ofp"], [])
                DMA("sp", sem_dbg, dbg_ml_d[tb * 128:(tb + 1) * 128, :], ofp[:, 512:1024], ["ofp"], [])
            TT("dve", sqb, ofp, ofp, ALU.mult, ["ofp"], ["sqb"])
            P.op("dve", lambda e: e.tensor_reduce(out=s16, in_=sqb.rearrange("p (h c) -> p h c", c=64), axis=AX.X,
                                                  op=ALU.add), ["sqb"], ["s16"])
            RSQ(r16, s16, 64 * EPS, 16, ["s16"], ["r16"])
            TT("dve", sqb.rearrange("p (h c) -> p h c", c=64), ofp.rearrange("p (h c) -> p h c", c=64),
               r16.unsqueeze(2).broadcast_to([128, 16, 64]), ALU.mult, ["ofp", "r16"], ["sqb"])
            STT("dve", yb, sqb, 8.0, sg, ALU.mult, ALU.mult, ["sqb", "sg"], [ytag])

        def st_B2(bi):
            xi = bi % 3
            xb = xbuf[xi]
            yb = ybr[bi % 2]
            ytag = "yb%d" % (bi % 2)
            x1 = x1r[bi % 2]
            x1tag = "x1_%d" % (bi % 2)
            tb = NB - 1 - bi
            P.group_begin()
            for kc in range(8):
                TR(pbf(0)[:, kc * 128:(kc + 1) * 128], yb[:, kc * 128:(kc + 1) * 128], [ytag], ["ps0"])
            CP("act", yT, pbf(0).rearrange("p (k t) -> p k t", t=128), ["ps0"], ["yT"])
            P.group_end()
            for half in range(2):
                for kc in range(8):
                    MM(psb[4 + half][:, :], yT[:, kc, :], Wout[:, kc, half * 512:(half + 1) * 512], kc == 0, kc == 7,
                       ["yT", "Wout"], ["ps%d" % (4 + half)])
            ACT(yb[:, 0:512], psb[4][:, :], AF.Square, ["ps4"], ["sA0", ytag], accum=sA0)
            ACT(yb[:, 512:1024], psb[5][:, :], AF.Square, ["ps5"], ["sA1", ytag], accum=sA1)
            RSQ2(rB, sA0, sA1, D * EPS, ["sA0", "sA1"], ["rB"])
            for half in range(2):
                STT("dve", x1[:, half * 512:(half + 1) * 512], psb[4 + half][:, :], rB[:, 0:1],
                    gpost_b[:, half * 512:(half + 1) * 512], ALU.mult, ALU.mult, ["ps%d" % (4 + half), "rB", "gpost_b"], [x1tag])
            TT("dve", x1, x1, xb, ALU.add, [x1tag, "xb%d" % xi], [x1tag])
            if DBG:
                DMA("sp", sem_dbg, dbg_x1_d[tb * 128:(tb + 1) * 128, :], x1, [x1tag], [])

        def st_C(bi):
            tb = NB - 1 - bi
            x1 = x1r[bi % 2]
            x1tag = "x1_%d" % (bi % 2)
            pl = plr[0]
            pltag = "pl"
            pb_ = pbuf[bi % 4]
            CP("act", x1b, x1, [x1tag], ["x1b"])
            CP("act", pbb, pb_, ["pb%d" % (bi % 4)], ["pbb"])
            P.group_begin()
            for kc in range(8):
                TR(pbf(0)[:, kc * 128:(kc + 1) * 128], x1b[:, kc * 128:(kc + 1) * 128], ["x1b"], ["ps0"])
            CP("act", x1T, pbf(0).rearrange("p (k t) -> p k t", t=128), ["ps0"], ["x1T"])
            P.group_end()
            P.group_begin()
            for kc in range(2):
                TR(pbf(0)[:, kc * 128:(kc + 1) * 128], pbb[:, kc * 128:(kc + 1) * 128], ["pbb"], ["ps0"])
            CP("dve", pT, pbf(0)[:, 0:256].rearrange("p (k t) -> p k t", t=128), ["ps0"], ["pT"])
            P.group_end()
            for half in range(2):
                for kc in range(8):
                    MM(psb[6 + half][:, :], x1T[:, kc, :], Wpg[:, kc, half * 512:(half + 1) * 512], kc == 0, False,
                       ["x1T", "Wpg"], ["ps%d" % (6 + half)])
                MM(psb[6 + half][:, :], ones_row[0:1, :], bpg_b[0:1, half * 512:(half + 1) * 512], False, True,
                   ["ones_row", "bpg_b"], ["ps%d" % (6 + half)])
            ACT(sg2[:, 0:512], psb[6][:, :], AF.Sigmoid, ["ps6"], ["sg2"])
            ACT(sg2[:, 512:1024], psb[7][:, :], AF.Sigmoid, ["ps7"], ["sg2"])
            for half in range(2):
                for kc in range(2):
                    MM(psb[6 + half][:, :], pT[:, kc, :], Wple[:, kc, half * 512:(half + 1) * 512], kc == 0, kc == 1,
                       ["pT", "Wple"], ["ps%d" % (6 + half)])
            ACT(x1b[:, 0:512], psb[6][:, :], AF.Square, ["ps6"], ["sC0", "x1b"], accum=sC0)
            ACT(x1b[:, 512:1024], psb[7][:, :], AF.Square, ["ps7"], ["sC1", "x1b"], accum=sC1)
            RSQ2(rD, sC0, sC1, D * EPS, ["sC0", "sC1"], ["rD"])
            for half in range(2):
                STT("dve", pl[:, half * 512:(half + 1) * 512], psb[6 + half][:, :], rD[:, 0:1],
                    gple_b[:, half * 512:(half + 1) * 512], ALU.mult, ALU.mult, ["ps%d" % (6 + half), "rD", "gple_b"], [pltag])
            if DBG:
                DMA("sp", sem_dbg, dbg_pl_d[tb * 128:(tb + 1) * 128, :], pl, [pltag], [])
                DMA("sp", sem_dbg, dbg_sg_d[tb * 128:(tb + 1) * 128, :], sg2, ["sg2"], [])
            TT("dve", pl, pl, sg2, ALU.mult, [pltag, "sg2"], [pltag])
            TT("dve", pl, pl, x1, ALU.add, [pltag, x1tag], [pltag])
            DMA("sp", sem_o[0], out_d[tb * 128:(tb + 1) * 128, :], pl, [pltag], [])

        stages = [st_A, st_B1, st_B2, st_C]
        for step in range(NB + len(stages) - 1):
            lists = []
            for k in range(len(stages) - 1, -1, -1):
                if 0 <= step - k < NB:
                    lists.append(P.collect(stages[k], step - k))
            if SEQ5:
                for l in lists:
                    P.replay_interleaved([l])
            else:
                P.replay_interleaved(lists)
        P.wait_all("sp", ["pl"])
        if DBG:
            P.wait_all("sp", ["ofp"])
        P.finish()
    return nc


_NC_CACHE = {}


def kernel(x, p, positions, norm_pre_g, w_in, q_norm_g, w_uq, kv_norm_g, w_ukv, sb_out_norm_g, mla_out_norm_g,
           w_out, norm_post_g, w_ple, ple_norm_g, w_ple_gate, b_ple_gate):
    n = 8
    x = np.asarray(x, dtype=np.float32)
    p = np.asarray(p, dtype=np.float32)
    positions = np.asarray(positions, dtype=np.int32)
    half = 16
    freq = (np.float32(10000.0) ** (-(np.arange(half, dtype=np.float32)) / np.float32(half))).astype(np.float32)
    freq32 = np.concatenate([freq, freq]).reshape(32, 1).astype(np.float32)
    shared = {
        "freq": freq32,
        "norm_pre_g": np.ascontiguousarray(np.asarray(norm_pre_g, np.float32)[0]),
        "w_in": np.ascontiguousarray(np.asarray(w_in, np.float32)[0]),
        "q_norm_g": np.ascontiguousarray(np.asarray(q_norm_g, np.float32)[0]),
        "w_uq": np.ascontiguousarray(np.asarray(w_uq, np.float32)[0]),
        "kv_norm_g": np.ascontiguousarray(np.asarray(kv_norm_g, np.float32)[0]),
        "w_ukv": np.ascontiguousarray(np.asarray(w_ukv, np.float32)[0]),
        "sb_out_norm_g": np.ascontiguousarray(np.asarray(sb_out_norm_g, np.float32)[0]),
        "mla_out_norm_g": np.ascontiguousarray(np.asarray(mla_out_norm_g, np.float32)[0]),
        "w_out": np.ascontiguousarray(np.asarray(w_out, np.float32)[0]),
        "norm_post_g": np.ascontiguousarray(np.asarray(norm_post_g, np.float32)[0]),
        "w_ple": np.ascontiguousarray(np.asarray(w_ple, np.float32)[0]),
        "ple_norm_g": np.ascontiguousarray(np.asarray(ple_norm_g, np.float32)[0]),
        "w_ple_gate": np.ascontiguousarray(np.asarray(w_ple_gate, np.float32)[0]),
        "b_ple_gate": np.ascontiguousarray(np.asarray(b_ple_gate, np.float32)[0]),
    }
    in_maps = []
    for b in range(n):
        m = dict(shared)
        m["x"] = np.ascontiguousarray(x[b, ::-1, :])
        m["p"] = np.ascontiguousarray(p[0, b, ::-1, :])
        m["pos"] = np.ascontiguousarray(positions[b, ::-1])
        in_maps.append(m)
    if "nc" not in _NC_CACHE:
        _NC_CACHE["nc"] = build()
    nc = _NC_CACHE["nc"]
    res = run_bass_kernel_spmd(nc, in_maps, core_ids=list(range(n)))
    out = np.stack([np.asarray(r["out"])[::-1, :] for r in res.results], axis=0)
    kernel.last_results = res.results
    return np.ascontiguousarray(out.astype(np.float32))
```
